# Optimizing a Trainium2 kernel written in Bass

```python
import math
import jax
import jax.numpy as jnp
from jax import lax
import numpy as np

D_MODEL = 1024
BATCH = 4
SEQ = 4096
DEPTH = 4

GRID_W = 64
CTX_LEN = 256
D_MIX = D_MODEL
D_FF = 2816
N_MOD = 9
EPS = 1e-6

GLA_H = 4
GLA_DK = 32
GLA_DV = 64
GLA_LORA = 16
GLA_TAU = 16.0
GLA_CHUNK = 64
SSM_H = 4
SSM_P = 64
SSM_G = 2
SSM_N = 64
SSM_CONV = 3
SSM_CHUNK = 64
SSM_DI = SSM_H * SSM_P
SSM_CONV_CH = SSM_DI + 2 * SSM_G * SSM_N
RWKV_H = 4
RWKV_N = 64
RWKV_D = RWKV_H * RWKV_N
RWKV_W_LORA = 64
RWKV_A_LORA = 64
RWKV_G_LORA = 128
RWKV_GN_EPS = 64e-5
ATT_HQ = 4
ATT_HKV = 2
ATT_GROUP = ATT_HQ // ATT_HKV
ATT_HD = 64
ATT_BLOCK = 128
ROPE_THETA = 10000.0
ROPE_AXIS_DIM = ATT_HD // 2

GLA_COLS = [GLA_H * GLA_DK, GLA_H * GLA_DK, GLA_H * GLA_DV, GLA_H * GLA_DV, GLA_LORA]
SSM_COLS = [SSM_DI, SSM_CONV_CH, SSM_H]
RWKV_COLS = [RWKV_D, RWKV_D, RWKV_D, RWKV_W_LORA, RWKV_A_LORA, RWKV_G_LORA]
ATT_COLS = [ATT_HQ * ATT_HD, ATT_HKV * ATT_HD, ATT_HKV * ATT_HD]
GROUP_COLS = [sum(GLA_COLS), sum(SSM_COLS), sum(RWKV_COLS), sum(ATT_COLS)]
N_IN = sum(GROUP_COLS)

kernel_name = 'hybrid_parallel_group_dit_trunk'


def split_cols(u, sizes):
    out, start = [], 0
    for s in sizes:
        out.append(u[..., start:start + s])
        start += s
    return out


def rms_norm(x, g, eps=EPS):
    xf = x.astype(jnp.float32)
    y = xf * lax.rsqrt(jnp.mean(xf * xf, axis=-1, keepdims=True) + eps)
    return (y * g.astype(jnp.float32)).astype(x.dtype)


def modulate(h, g, shift, scale):
    return rms_norm(h, g) * (1 + scale[:, None]) + shift[:, None]


def swiglu(h, w_in, w_out):
    a, b = jnp.split(h @ w_in, 2, axis=-1)
    return (jax.nn.silu(a) * b) @ w_out


def _ident(t):
    return t


def _flip(t):
    return jnp.flip(t, axis=1)


def shift_prev(u):
    return jnp.pad(u, ((0, 0), (1, 0), (0, 0)))[:, :-1]


def shift_next(u):
    return jnp.pad(u, ((0, 0), (0, 1), (0, 0)))[:, 1:]


def dwconv_centred(u, w, b):
    pad = w.shape[0] // 2
    y = lax.conv_general_dilated(u, w[:, None, :].astype(u.dtype), window_strides=(1,),
                                 padding=[(pad, pad)], dimension_numbers=('NWC', 'WIO', 'NWC'),
                                 feature_group_count=u.shape[-1])
    return y + b


def axial_rope_tables(n_tokens, dtype):
    rows = n_tokens // GRID_W
    f32 = jnp.float32
    row = jnp.repeat(jnp.arange(rows, dtype=f32), GRID_W)
    col = jnp.tile(jnp.arange(GRID_W, dtype=f32), rows)
    inv = ROPE_THETA ** (-jnp.arange(0, ROPE_AXIS_DIM, 2, dtype=f32) / ROPE_AXIS_DIM)
    ang = jnp.stack([row[:, None] * inv, col[:, None] * inv], axis=1)
    return jnp.cos(ang).astype(dtype), jnp.sin(ang).astype(dtype)


def apply_axial_rope(t, cos, sin):
    bsz, n, h, d = t.shape
    t = t.reshape(bsz, n, h, 2, 2, ROPE_AXIS_DIM // 2)
    t1, t2 = t[..., 0, :], t[..., 1, :]
    c, s = cos[None, :, None], sin[None, :, None]
    out = jnp.stack([t1 * c - t2 * s, t1 * s + t2 * c], axis=-2)
    return out.reshape(bsz, n, h, d)


def gla_chunked(q, k, v, log_a, s0):
    bsz, t, h, dk = q.shape
    dv = v.shape[-1]
    cs = GLA_CHUNK
    nc = t // cs
    q, k, v, log_a = [z.reshape(bsz, nc, cs, h, z.shape[-1]) for z in (q, k, v, log_a)]
    b = jnp.cumsum(log_a, axis=2)
    b_last = b[:, :, -1:]
    q_dec = q * jnp.exp(b)
    causal = jnp.tril(jnp.ones((cs, cs), bool))
    att = jnp.einsum('bnthd,bnshd->bnhts', q_dec, k * jnp.exp(-b))
    att = jnp.where(causal, att, 0.0)
    o_intra = jnp.einsum('bnhts,bnshv->bnthv', att, v)
    chunk_state = jnp.einsum('bnshd,bnshv->bnhdv', k * jnp.exp(b_last - b), v)
    chunk_decay = jnp.exp(b_last[:, :, 0])

    def step(s, inp):
        st, dec = inp
        return s * dec[..., None] + st, s

    s_fin, s_prev = lax.scan(step, s0, (jnp.moveaxis(chunk_state, 1, 0), jnp.moveaxis(chunk_decay, 1, 0)))
    s_prev = jnp.moveaxis(s_prev, 0, 1)
    o_inter = jnp.einsum('bnthd,bnhdv->bnthv', q_dec, s_prev)
    return (o_intra + o_inter).reshape(bsz, t, h, dv), s_fin


def gla_mixer(u_c, u_x, w_dec, b_dec, norm_g, need_ctx):
    f32 = jnp.float32

    def streams(u):
        bsz, t, _ = u.shape
        q, k, v, g, lr = split_cols(u, GLA_COLS)
        q = q.reshape(bsz, t, GLA_H, GLA_DK).astype(f32) * (GLA_DK ** -0.5)
        k = k.reshape(bsz, t, GLA_H, GLA_DK).astype(f32)
        v = v.reshape(bsz, t, GLA_H, GLA_DV).astype(f32)
        return q, k, v, g, lr

    def log_decay(lr, d):
        z = (lr @ w_dec[d] + b_dec[d]).astype(f32)
        return (jax.nn.log_sigmoid(z) / GLA_TAU).reshape(lr.shape[0], lr.shape[1], GLA_H, GLA_DK)

    def finish(o, g):
        bsz, t = o.shape[:2]
        o = rms_norm(o, norm_g).reshape(bsz, t, GLA_H * GLA_DV)
        return o.astype(g.dtype) * jax.nn.silu(g)

    qc, kc, vc, gc, lc = streams(u_c)
    qx, kx, vx, gx, lx = streams(u_x)
    bsz = u_x.shape[0]
    o_c = jnp.zeros_like(vc)
    o_x = jnp.zeros_like(vx)
    for d in range(2):
        fl = _flip if d else _ident
        s0 = jnp.zeros((bsz, GLA_H, GLA_DK, GLA_DV), f32)
        oc, s_ctx = gla_chunked(fl(qc), fl(kc), fl(vc), fl(log_decay(lc, d)), s0)
        ox, _ = gla_chunked(fl(qx), fl(kx), fl(vx), fl(log_decay(lx, d)), s_ctx)
        o_c = o_c + fl(oc)
        o_x = o_x + fl(ox)
    y_c = finish(o_c, gc) if need_ctx else None
    return y_c, finish(o_x, gx)


def ssd_chunked(xs, dt, a, bm, cm, h0):
    bsz, t, h, p = xs.shape
    g, n = bm.shape[-2:]
    r = h // g
    cs = SSM_CHUNK
    nc = t // cs
    xdt = (xs * dt[..., None]).reshape(bsz, nc, cs, g, r, p)
    cum = jnp.cumsum((dt * a).reshape(bsz, nc, cs, g, r), axis=2)
    bc = bm.reshape(bsz, nc, cs, g, n)
    cc = cm.reshape(bsz, nc, cs, g, n)
    cum_t = jnp.moveaxis(cum, 2, -1)
    mask = jnp.tril(jnp.ones((cs, cs), bool))
    seg = jnp.exp(jnp.where(mask, cum_t[..., :, None] - cum_t[..., None, :], -jnp.inf))
    cb = jnp.einsum('bctgn,bcsgn->bcgts', cc, bc)
    y_intra = jnp.einsum('bcgts,bcgrts,bcsgrp->bctgrp', cb, seg, xdt)
    last = cum[:, :, -1]
    to_end = jnp.exp(last[:, :, None] - cum)
    chunk_state = jnp.einsum('bcsgn,bcsgr,bcsgrp->bcgrpn', bc, to_end, xdt)

    def step(hs, inp):
        st, dec = inp
        return hs * dec[..., None, None] + st, hs

    h_fin, h_prev = lax.scan(step, h0, (jnp.moveaxis(chunk_state, 1, 0), jnp.moveaxis(jnp.exp(last), 1, 0)))
    h_prev = jnp.moveaxis(h_prev, 0, 1)
    y_inter = jnp.einsum('bctgn,bctgr,bcgrpn->bctgrp', cc, jnp.exp(cum), h_prev)
    return (y_intra + y_inter).reshape(bsz, t, h, p), h_fin


def mamba_mixer(u_c, u_x, conv_w, conv_b, dt_bias, a_log, d_skip, norm_g, need_ctx):
    f32 = jnp.float32

    def streams(u):
        bsz, t, _ = u.shape
        z, xbc, dt = split_cols(u, SSM_COLS)
        xbc = jax.nn.silu(dwconv_centred(xbc, conv_w, conv_b))
        xs, bm, cm = split_cols(xbc, [SSM_DI, SSM_G * SSM_N, SSM_G * SSM_N])
        return (z, xs.reshape(bsz, t, SSM_H, SSM_P).astype(f32),
                bm.reshape(bsz, t, SSM_G, SSM_N).astype(f32),
                cm.reshape(bsz, t, SSM_G, SSM_N).astype(f32), dt.astype(f32))

    def finish(y, z, xs):
        bsz, t = y.shape[:2]
        y = y + d_skip.astype(f32)[:, None] * xs
        y = y.reshape(bsz, t, SSM_DI) * jax.nn.silu(z.astype(f32))
        return rms_norm(y, norm_g).astype(z.dtype)

    zc, xc, bc, cc, dtc = streams(u_c)
    zx, xx, bx, cx, dtx = streams(u_x)
    bsz = u_x.shape[0]
    y_c = jnp.zeros_like(xc)
    y_x = jnp.zeros_like(xx)
    for d in range(2):
        fl = _flip if d else _ident
        a = -jnp.exp(a_log[d].astype(f32))
        bias = dt_bias[d].astype(f32)
        h0 = jnp.zeros((bsz, SSM_G, SSM_H // SSM_G, SSM_P, SSM_N), f32)
        yc, h_ctx = ssd_chunked(fl(xc), fl(jax.nn.softplus(dtc + bias)), a, fl(bc), fl(cc), h0)
        yx, _ = ssd_chunked(fl(xx), fl(jax.nn.softplus(dtx + bias)), a, fl(bx), fl(cx), h_ctx)
        y_c = y_c + fl(yc)
        y_x = y_x + fl(yx)
    out_c = finish(y_c, zc, xc) if need_ctx else None
    return out_c, finish(y_x, zx, xx)


def rwkv7_scan(r, w, k, v, a_in, b_in, s0):
    def step(s, inp):
        r_t, w_t, k_t, v_t, a_t, b_t = inp
        sa = jnp.einsum('bhij,bhj->bhi', s, a_t)
        s = s * w_t[:, :, None, :] + sa[..., None] * b_t[:, :, None, :] + v_t[..., None] * k_t[:, :, None, :]
        return s, jnp.einsum('bhij,bhj->bhi', s, r_t)

    seq = tuple(jnp.moveaxis(z, 1, 0) for z in (r, w, k, v, a_in, b_in))
    s_fin, y = lax.scan(step, s0, seq)
    return jnp.moveaxis(y, 0, 1), s_fin


def rwkv_mixer(u_c, u_x, shift_mu, w0, w_dec, a0, w_a, w_g, k_k, k_a, r_k, ln_g, ln_b, need_ctx):
    f32 = jnp.float32

    def streams(u):
        bsz, t, _ = u.shape
        u = u + shift_mu[0] * (shift_prev(u) - u) + shift_mu[1] * (shift_next(u) - u)
        r, k, v, lw, la, lg = split_cols(u, RWKV_COLS)
        heads = lambda z: z.reshape(bsz, t, RWKV_H, RWKV_N).astype(f32)
        a = jax.nn.sigmoid(a0 + la @ w_a)
        g = jax.nn.sigmoid(lg) @ w_g
        kk = heads(k * k_k)
        kk = kk / jnp.maximum(jnp.sqrt(jnp.sum(kk * kk, axis=-1, keepdims=True)), 1e-12)
        k = k * (1 + (a - 1) * k_a)
        return heads(r), heads(k), heads(v), heads(a), kk, jnp.tanh(lw), g

    def log_decay(tlw, d):
        wr = (w0[d] + tlw @ w_dec[d]).astype(f32)
        wr = -jax.nn.softplus(-wr) - 0.5
        return (-jnp.exp(wr)).reshape(tlw.shape[0], tlw.shape[1], RWKV_H, RWKV_N)

    def finish(y, r, k, v, g):
        bsz, t = y.shape[:2]
        mu = jnp.mean(y, axis=-1, keepdims=True)
        var = jnp.mean(jnp.square(y - mu), axis=-1, keepdims=True)
        y = ((y - mu) * lax.rsqrt(var + RWKV_GN_EPS)).reshape(bsz, t, RWKV_D) * ln_g + ln_b
        bonus = jnp.sum(r * k * r_k.astype(f32), axis=-1, keepdims=True) * v
        y = y + bonus.reshape(bsz, t, RWKV_D)
        return (y * g.astype(f32)).astype(g.dtype)

    rc, kc, vc, ac, kkc, tc, gc = streams(u_c)
    rx, kx, vx, ax, kkx, tx, gx = streams(u_x)
    bsz = u_x.shape[0]
    y_c = jnp.zeros_like(vc)
    y_x = jnp.zeros_like(vx)
    for d in range(2):
        fl = _flip if d else _ident
        s0 = jnp.zeros((bsz, RWKV_H, RWKV_N, RWKV_N), f32)
        yc, s_ctx = rwkv7_scan(fl(rc), fl(jnp.exp(log_decay(tc, d))), fl(kc), fl(vc), fl(-kkc), fl(kkc * ac), s0)
        yx, _ = rwkv7_scan(fl(rx), fl(jnp.exp(log_decay(tx, d))), fl(kx), fl(vx), fl(-kkx), fl(kkx * ax), s_ctx)
        y_c = y_c + fl(yc)
        y_x = y_x + fl(yx)
    out_c = finish(y_c, rc, kc, vc, gc) if need_ctx else None
    return out_c, finish(y_x, rx, kx, vx, gx)


def gqa_attend(q, k, v):
    s = jnp.einsum('bqkgd,bskd->bkgqs', q, k).astype(jnp.float32) * (ATT_HD ** -0.5)
    p = jax.nn.softmax(s, axis=-1)
    return jnp.einsum('bkgqs,bskd->bqkgd', p.astype(v.dtype), v)


def attention_mixer(u_c, u_x, q_norm, k_norm, cos, sin, need_ctx):
    def streams(u):
        bsz, t, _ = u.shape
        q, k, v = split_cols(u, ATT_COLS)
        q = rms_norm(q.reshape(bsz, t, ATT_HQ, ATT_HD), q_norm)
        k = rms_norm(k.reshape(bsz, t, ATT_HKV, ATT_HD), k_norm)
        return q, k, v.reshape(bsz, t, ATT_HKV, ATT_HD)

    qc, kc, vc = streams(u_c)
    qx, kx, vx = streams(u_x)
    qx = apply_axial_rope(qx, cos, sin)
    kx = apply_axial_rope(kx, cos, sin)
    keys = jnp.concatenate([kx, kc], axis=1)
    vals = jnp.concatenate([vx, vc], axis=1)
    bsz, t = qx.shape[:2]
    nb = t // ATT_BLOCK
    q_blocks = jnp.moveaxis(qx.reshape(bsz, nb, ATT_BLOCK, ATT_HKV, ATT_GROUP, ATT_HD), 1, 0)
    y_x = lax.map(lambda qb: gqa_attend(qb, keys, vals), q_blocks)
    y_x = jnp.moveaxis(y_x, 0, 1).reshape(bsz, t, ATT_HQ * ATT_HD)
    y_c = None
    if need_ctx:
        lc = qc.shape[1]
        y_c = gqa_attend(qc.reshape(bsz, lc, ATT_HKV, ATT_GROUP, ATT_HD), kc, vc).reshape(bsz, lc, ATT_HQ * ATT_HD)
    return y_c, y_x


def hybrid_layer(x, ctx, mod_x, mod_c, cos, sin, p, need_ctx):
    mx = jnp.split(mod_x, N_MOD, axis=-1)
    mc = jnp.split(mod_c, N_MOD, axis=-1)
    x = x + 0.5 * mx[2][:, None] * swiglu(modulate(x, p['norm_g'][0], mx[0], mx[1]), p['ffn_in'][0], p['ffn_out'][0])
    ctx = ctx + 0.5 * mc[2][:, None] * swiglu(modulate(ctx, p['norm_g'][0], mc[0], mc[1]), p['ffn_in'][0], p['ffn_out'][0])
    ux = modulate(x, p['norm_g'][1], mx[3], mx[4]) @ p['w_in']
    uc = modulate(ctx, p['norm_g'][1], mc[3], mc[4]) @ p['w_in']
    gx = split_cols(ux, GROUP_COLS)
    gc = split_cols(uc, GROUP_COLS)
    gla_c, gla_x = gla_mixer(gc[0], gx[0], p['gla_w_dec'], p['gla_b_dec'], p['gla_norm'], need_ctx)
    ssm_c, ssm_x = mamba_mixer(gc[1], gx[1], p['ssm_conv_w'], p['ssm_conv_b'], p['ssm_dt_bias'],
                               p['ssm_a_log'], p['ssm_d'], p['ssm_norm'], need_ctx)
    rw_c, rw_x = rwkv_mixer(gc[2], gx[2], p['rwkv_shift_mu'], p['rwkv_w0'], p['rwkv_w_dec'], p['rwkv_a0'],
                            p['rwkv_w_a'], p['rwkv_w_g'], p['rwkv_k_k'], p['rwkv_k_a'], p['rwkv_r_k'],
                            p['rwkv_ln_g'], p['rwkv_ln_b'], need_ctx)
    at_c, at_x = attention_mixer(gc[3], gx[3], p['att_q_norm'], p['att_k_norm'], cos, sin, need_ctx)
    x = x + mx[5][:, None] * (jnp.concatenate([gla_x, ssm_x, rw_x, at_x], axis=-1) @ p['w_out'])
    if need_ctx:
        ctx = ctx + mc[5][:, None] * (jnp.concatenate([gla_c, ssm_c, rw_c, at_c], axis=-1) @ p['w_out'])
        ctx = ctx + 0.5 * mc[8][:, None] * swiglu(modulate(ctx, p['norm_g'][2], mc[6], mc[7]), p['ffn_in'][1], p['ffn_out'][1])
    x = x + 0.5 * mx[8][:, None] * swiglu(modulate(x, p['norm_g'][2], mx[6], mx[7]), p['ffn_in'][1], p['ffn_out'][1])
    return x, ctx


def setup_inputs(seed: int = 0) -> dict:
    key = jax.random.key(seed)
    keys = jax.random.split(key, 40)
    counter = [0]
    f32 = jnp.float32

    def nk():
        k = keys[counter[0]]
        counter[0] += 1
        return k

    def nrm(shape, scale):
        return jax.random.normal(nk(), shape, f32) * scale

    def unif(shape, lo, hi):
        return jax.random.uniform(nk(), shape, f32, lo, hi)

    L, D = DEPTH, D_MODEL
    x = nrm((BATCH, SEQ, D), 1.0)
    c = nrm((BATCH, D), 1.0)
    ctx = nrm((BATCH, CTX_LEN, D), 1.0)
    c_ctx = nrm((D,), 1.0)
    norm_g = 1.0 + nrm((L, 3, D), 0.1)
    w_mod = nrm((L, D, N_MOD * D), 0.5 * D ** -0.5)
    b_mod = nrm((L, N_MOD * D), 0.02)
    ffn_in = nrm((L, 2, D, 2 * D_FF), D ** -0.5)
    ffn_out = nrm((L, 2, D_FF, D), D_FF ** -0.5)
    w_in = nrm((L, D, N_IN), D ** -0.5)
    w_out = nrm((L, D_MIX, D), D_MIX ** -0.5)
    gla_w_dec = nrm((L, 2, GLA_LORA, GLA_H * GLA_DK), GLA_LORA ** -0.5)
    gla_b_dec = nrm((L, 2, GLA_H * GLA_DK), 0.5)
    gla_norm = 1.0 + nrm((L, GLA_DV), 0.1)
    ssm_conv_w = nrm((L, SSM_CONV, SSM_CONV_CH), SSM_CONV ** -0.5)
    ssm_conv_b = nrm((L, SSM_CONV_CH), 0.02)
    dt0 = jnp.exp(unif((L, 2, SSM_H), math.log(1e-3), math.log(1e-1)))
    ssm_dt_bias = dt0 + jnp.log(-jnp.expm1(-dt0))
    ssm_a_log = jnp.log(unif((L, 2, SSM_H), 1.0, 16.0))
    ssm_d = 1.0 + nrm((L, SSM_H), 0.1)
    ssm_norm = 1.0 + nrm((L, SSM_DI), 0.1)
    rwkv_shift_mu = unif((L, 2, sum(RWKV_COLS)), 0.0, 0.5)
    rwkv_w0 = unif((L, 2, RWKV_D), -3.0, 1.0)
    rwkv_w_dec = nrm((L, 2, RWKV_W_LORA, RWKV_D), 0.5 * RWKV_W_LORA ** -0.5)
    rwkv_a0 = nrm((L, RWKV_D), 0.1)
    rwkv_w_a = nrm((L, RWKV_A_LORA, RWKV_D), RWKV_A_LORA ** -0.5)
    rwkv_w_g = nrm((L, RWKV_G_LORA, RWKV_D), RWKV_G_LORA ** -0.5)
    rwkv_k_k = 0.85 + nrm((L, RWKV_D), 0.05)
    rwkv_k_a = 1.0 + nrm((L, RWKV_D), 0.05)
    rwkv_r_k = nrm((L, RWKV_H, RWKV_N), 0.1)
    rwkv_ln_g = 1.0 + nrm((L, RWKV_D), 0.1)
    rwkv_ln_b = nrm((L, RWKV_D), 0.02)
    att_q_norm = 1.0 + nrm((L, ATT_HD), 0.1)
    att_k_norm = 1.0 + nrm((L, ATT_HD), 0.1)
    return {'x': x, 'c': c, 'ctx': ctx, 'c_ctx': c_ctx, 'norm_g': norm_g, 'w_mod': w_mod, 'b_mod': b_mod,
            'ffn_in': ffn_in, 'ffn_out': ffn_out, 'w_in': w_in, 'w_out': w_out,
            'gla_w_dec': gla_w_dec, 'gla_b_dec': gla_b_dec, 'gla_norm': gla_norm,
            'ssm_conv_w': ssm_conv_w, 'ssm_conv_b': ssm_conv_b, 'ssm_dt_bias': ssm_dt_bias,
            'ssm_a_log': ssm_a_log, 'ssm_d': ssm_d, 'ssm_norm': ssm_norm,
            'rwkv_shift_mu': rwkv_shift_mu, 'rwkv_w0': rwkv_w0, 'rwkv_w_dec': rwkv_w_dec, 'rwkv_a0': rwkv_a0,
            'rwkv_w_a': rwkv_w_a, 'rwkv_w_g': rwkv_w_g, 'rwkv_k_k': rwkv_k_k, 'rwkv_k_a': rwkv_k_a,
            'rwkv_r_k': rwkv_r_k, 'rwkv_ln_g': rwkv_ln_g, 'rwkv_ln_b': rwkv_ln_b,
            'att_q_norm': att_q_norm, 'att_k_norm': att_k_norm}


def reference(x, c, ctx, c_ctx, norm_g, w_mod, b_mod, ffn_in, ffn_out, w_in, w_out,
              gla_w_dec, gla_b_dec, gla_norm, ssm_conv_w, ssm_conv_b, ssm_dt_bias, ssm_a_log, ssm_d, ssm_norm,
              rwkv_shift_mu, rwkv_w0, rwkv_w_dec, rwkv_a0, rwkv_w_a, rwkv_w_g, rwkv_k_k, rwkv_k_a, rwkv_r_k,
              rwkv_ln_g, rwkv_ln_b, att_q_norm, att_k_norm):
    cos, sin = axial_rope_tables(x.shape[1], x.dtype)
    sc = jax.nn.silu(c)
    scc = jax.nn.silu(c_ctx)[None]
    for l in range(DEPTH):
        mod_x = sc @ w_mod[l] + b_mod[l]
        mod_c = scc @ w_mod[l] + b_mod[l]
        p = dict(norm_g=norm_g[l], ffn_in=ffn_in[l], ffn_out=ffn_out[l], w_in=w_in[l], w_out=w_out[l],
                 gla_w_dec=gla_w_dec[l], gla_b_dec=gla_b_dec[l], gla_norm=gla_norm[l],
                 ssm_conv_w=ssm_conv_w[l], ssm_conv_b=ssm_conv_b[l], ssm_dt_bias=ssm_dt_bias[l],
                 ssm_a_log=ssm_a_log[l], ssm_d=ssm_d[l], ssm_norm=ssm_norm[l],
                 rwkv_shift_mu=rwkv_shift_mu[l], rwkv_w0=rwkv_w0[l], rwkv_w_dec=rwkv_w_dec[l],
                 rwkv_a0=rwkv_a0[l], rwkv_w_a=rwkv_w_a[l], rwkv_w_g=rwkv_w_g[l], rwkv_k_k=rwkv_k_k[l],
                 rwkv_k_a=rwkv_k_a[l], rwkv_r_k=rwkv_r_k[l], rwkv_ln_g=rwkv_ln_g[l], rwkv_ln_b=rwkv_ln_b[l],
                 att_q_norm=att_q_norm[l], att_k_norm=att_k_norm[l])
        x, ctx = hybrid_layer(x, ctx, mod_x, mod_c, cos, sin, p, l < DEPTH - 1)
    return x
```

```python
import math
import numpy as np
import concourse.bass as bass
import concourse.mybir as mybir
from concourse.bass_utils import run_bass_kernel_spmd

F32 = mybir.dt.float32
BF16 = mybir.dt.bfloat16
I32 = mybir.dt.int32
AF = mybir.ActivationFunctionType
ALU = mybir.AluOpType
AX = mybir.AxisListType


class Buf:
    __slots__ = ("name", "w", "r", "t")

    def __init__(self, name, t=None):
        self.name = name
        self.w = None
        self.r = {}
        self.t = t

    def __getitem__(self, idx):
        return self.t[idx]


class Prog:
    ENGS = ("pe", "dve", "act", "pool", "sp")
    ROT = 30000

    def __init__(self, nc):
        self.nc = nc
        self.stack = []
        self.tstack = []
        self.streams = {e: [] for e in self.ENGS}
        self.sems = {}
        self.cur = {}
        self.waited = {e: {} for e in self.ENGS}
        self.nsem = 0
        for e in self.ENGS:
            self._new_eng_sem(e)
        self.dma_pool = {}
        self.dma_rr = {}
        for q, n in (("sp", 12), ("pool", 6), ("act", 6)):
            self.dma_pool[q] = [[self._new_sem("d%s%d" % (q, i)), 0] for i in range(n)]
            self.dma_rr[q] = 0
        self.ninst = 0

    def _new_sem(self, name):
        cm = self.nc.semaphore("s%d_%s" % (self.nsem, name))
        h = cm.__enter__()
        self.stack.append(cm)
        key = self.nsem
        self.nsem += 1
        self.sems[key] = h
        return key

    def _new_eng_sem(self, e):
        self.cur[e] = [self._new_sem(e), 0]

    def alloc_sb(self, name, shape, dt=F32):
        self.nalloc = getattr(self, "nalloc", 0) + 1
        cm = self.nc.sbuf_tensor("sb%d_%s" % (self.nalloc, name), list(shape), dt)
        t = cm.__enter__()
        self.tstack.append(cm)
        return Buf(name, t)

    def alloc_ps(self, name, shape, dt=F32):
        cm = self.nc.psum_tensor("ps_" + name, list(shape), dt)
        t = cm.__enter__()
        self.stack.append(cm)
        return Buf(name, t)

    def mark(self):
        return len(self.tstack)

    def barrier(self):
        toks = []
        for e in self.ENGS:
            k, v = self.cur[e]
            if v > 0:
                toks.append((k, v))
        for q, pool in self.dma_pool.items():
            for k, v in pool:
                if v > 0:
                    toks.append((k, v))
        for e in self.ENGS:
            waits = []
            for k, v in toks:
                if self.waited[e].get(k, 0) >= v:
                    continue
                self.waited[e][k] = v
                waits.append((k, v))
            self.streams[e].append((waits, None, None, 0))

    def release(self, mark):
        self.barrier()
        while len(self.tstack) > mark:
            cm = self.tstack.pop()
            cm.__exit__(None, None, None)

    def dram(self, name, shape, dt=F32, kind="Internal", addr_space=None):
        if addr_space is not None:
            t = self.nc.dram_tensor(name, list(shape), dt, kind=kind, addr_space=addr_space)
        else:
            t = self.nc.dram_tensor(name, list(shape), dt, kind=kind)
        return Buf(name, t.ap())

    def _collect(self, e, reads, writes, skip_same=False):
        need = {}

        def add(tok):
            if tok is None:
                return
            k, v = tok
            if need.get(k, 0) < v:
                need[k] = v
        for b in reads:
            add(b.w)
        for b in writes:
            add(b.w)
            for k, v in b.r.items():
                add((k, v))
        out = []
        wd = self.waited[e]
        for k, v in need.items():
            if skip_same and k == self.cur[e][0]:
                continue
            if wd.get(k, 0) >= v:
                continue
            wd[k] = v
            out.append((k, v))
        return out

    def _mark(self, tok, reads, writes):
        k, v = tok
        for b in reads:
            if b.r.get(k, 0) < v:
                b.r[k] = v
        for b in writes:
            b.w = tok
            b.r = {}

    def op(self, e, fn, reads=(), writes=(), skip_same=False):
        waits = self._collect(e, reads, writes, skip_same)
        cur = self.cur[e]
        if cur[1] >= self.ROT:
            self._new_eng_sem(e)
            cur = self.cur[e]
        cur[1] += 1
        tok = (cur[0], cur[1])
        self.streams[e].append((waits, fn, tok[0], 1))
        self._mark(tok, reads, writes)
        self.waited[e][tok[0]] = max(self.waited[e].get(tok[0], 0), 0)
        self.ninst += 1
        return tok

    def dma(self, q, fn, reads=(), writes=()):
        pool = self.dma_pool[q]
        i = self.dma_rr[q]
        self.dma_rr[q] = (i + 1) % len(pool)
        slot = pool[i]
        waits = self._collect(q, reads, writes)
        if slot[1] > 0 and self.waited[q].get(slot[0], 0) < slot[1]:
            self.waited[q][slot[0]] = slot[1]
            waits.append((slot[0], slot[1]))
        slot[1] += 16
        tok = (slot[0], slot[1])
        self.streams[q].append((waits, fn, tok[0], 16))
        self._mark(tok, reads, writes)
        self.ninst += 1
        return tok

    def finish(self):
        fin = []
        for q, pool in self.dma_pool.items():
            for k, v in pool:
                if v > 0:
                    fin.append((k, v))
        self.streams["sp"].append((fin, None, None, 0))
        nc = self.nc
        sems = self.sems
        streams = self.streams
        with nc.Block() as block:
            def emit(eng, lst):
                for waits, fn, sk, inc in lst:
                    for k, v in waits:
                        eng.wait_ge(sems[k], v)
                    if fn is not None:
                        fn(eng).then_inc(sems[sk], inc)

            @block.tensor
            def _(eng):
                emit(eng, streams["pe"])

            @block.vector
            def _(eng):
                emit(eng, streams["dve"])

            @block.scalar
            def _(eng):
                emit(eng, streams["act"])

            @block.gpsimd
            def _(eng):
                emit(eng, streams["pool"])

            @block.sync
            def _(eng):
                emit(eng, streams["sp"])
        for cm in reversed(self.tstack):
            cm.__exit__(None, None, None)
        for cm in reversed(self.stack):
            cm.__exit__(None, None, None)
        self.stack = []
        self.tstack = []


def view(buf, ap):
    return Buf(buf.name + "_v", ap)


TC = 2176
DFF = 2816
NJ = 22


def token_blocks():
    blks = [(0, 128, 0)]
    for i in range(8):
        blks.append((128 + 256 * i, 256, 1))
    return blks


def consts(p):
    c = {}
    c["ones"] = p.alloc_sb("ones", [128, 128], F32)
    p.op("pool", lambda e: e.memset(c["ones"][:], 1.0), writes=[c["ones"]])
    return c


def compute_mod(p, K, io, mlist, stage, bankM):
    scT = p.alloc_sb("scT", [128, 8, 2], F32)
    bmT = p.alloc_sb("bmT", [128, 72], F32)
    modT = p.alloc_sb("modT", [128, 9, 8, 2], F32)
    p.dma("sp", lambda e: e.dma_start(out=scT[:].rearrange("p k s -> p (k s)"), in_=io["cT"][:]), reads=[io["cT"]], writes=[scT])
    p.dma("sp", lambda e: e.dma_start(out=bmT[:], in_=io["bmT"][:]), reads=[io["bmT"]], writes=[bmT])
    p.op("act", lambda e: e.activation(out=scT[:], in_=scT[:], func=AF.Silu), reads=[scT], writes=[scT])
    si = 0
    for m in mlist:
        for kc in range(8):
            st = stage[si % 2]
            si += 1
            p.dma("sp", lambda e, st=st, kc=kc, m=m: e.dma_start(out=st[:, 0:1024], in_=io["w_mod"][kc * 128:(kc + 1) * 128, m * 1024:(m + 1) * 1024]),
                  reads=[io["w_mod"]], writes=[st])
            for j in range(8):
                p.op("pe", lambda e, st=st, kc=kc, j=j: e.matmul(bankM[:, j * 2:j * 2 + 2], st[:, j * 128:(j + 1) * 128], scT[:, kc, :],
                                                                start=(kc == 0 and j == 0), stop=(kc == 7 and j == 7), skip_group_check=True),
                     reads=[st, scT], writes=[bankM], skip_same=True)
        p.op("dve", lambda e, m=m: e.tensor_tensor(out=modT[:, m], in0=bankM[:, 0:16].rearrange("p (j s) -> p j s", s=2),
                                                  in1=bmT[:, m * 8:(m + 1) * 8].unsqueeze(2).broadcast_to([128, 8, 2]), op=ALU.add),
             reads=[bankM, bmT], writes=[modT])
    return modT


def load_cast(p, dst_ap_fn, src_ap_fn, ncols, stage, sidx, src_buf, dst_buf, q="sp"):
    st = stage[sidx % 2]
    p.dma(q, lambda e: e.dma_start(out=st[:, 0:ncols], in_=src_ap_fn()), reads=[src_buf], writes=[st])
    p.op("pool", lambda e: e.tensor_copy(out=dst_ap_fn(), in_=st[:, 0:ncols]), reads=[st], writes=[dst_buf])


def build_k1(pre_mix, post_mod, slot):
    nc = bass.Bass("TRN2", target_bir_lowering=False)
    p = Prog(nc)
    io = {}
    io["xT"] = p.dram("xT", [1024, TC], F32, "ExternalInput")
    io["cT"] = p.dram("cT", [128, 16], F32, "ExternalInput")
    io["bmT"] = p.dram("bmT", [128, 72], F32, "ExternalInput")
    io["ngT"] = p.dram("ngT", [128, 24], F32, "ExternalInput")
    io["w_mod"] = p.dram("w_mod", [1024, 9216], F32, "ExternalInput")
    io["ffn_in"] = p.dram("ffn_in", [1024, 2 * DFF], F32, "ExternalInput")
    io["ffn_out"] = p.dram("ffn_out", [DFF, 1024], F32, "ExternalInput")
    io["yT"] = p.dram("yT", [1024, TC], F32, "ExternalOutput")
    if pre_mix:
        io["P0"] = p.dram("P0", [1024, TC], F32, "ExternalInput")
        io["P1"] = p.dram("P1", [1024, TC], F32, "ExternalInput")
    if post_mod:
        io["xmT"] = p.dram("xmT", [1024, TC], BF16, "ExternalOutput")
    token_pass(p, io, pre_mix, post_mod, slot)
    p.finish()
    return nc


def token_pass(p, io, pre_mix, post_mod, slot, G=None, blocks=None):
    if G is None:
        K = consts(p)
        banks = [p.alloc_ps("bank%d" % i, [128, 512], F32) for i in range(8)]
    else:
        K = {"ones": G["ones"]}
        banks = G["banks"]
    if blocks is None:
        blocks = token_blocks()
    stage = [p.alloc_sb("stage%d" % i, [128, 1408], F32) for i in range(2)]
    mbase = 0 if slot == 0 else 6
    nidx = 0 if slot == 0 else 2
    mlist = [mbase, mbase + 1, mbase + 2]
    if pre_mix:
        mlist.append(5)
    if post_mod:
        mlist += [3, 4]
    modT = compute_mod(p, K, io, mlist, stage, banks[7])
    ngT = p.alloc_sb("ngT", [128, 3, 8], F32)
    p.dma("sp", lambda e: e.dma_start(out=ngT[:].rearrange("p n k -> p (n k)"), in_=io["ngT"][:]), reads=[io["ngT"]], writes=[ngT])
    geff = p.alloc_sb("geff", [128, 8, 2], F32)
    hg = p.alloc_sb("hg", [128, 8, 2], F32)

    def mk_geff(dst, n, msc):
        p.op("dve", lambda e: e.tensor_scalar(out=dst[:], in0=modT[:, msc], scalar1=1.0, scalar2=None, op0=ALU.add), reads=[modT], writes=[dst])
        p.op("dve", lambda e: e.tensor_tensor(out=dst[:], in0=dst[:], in1=ngT[:, n, :].unsqueeze(2).broadcast_to([128, 8, 2]), op=ALU.mult), reads=[dst, ngT], writes=[dst])
    mk_geff(geff, nidx, mbase + 1)
    p.op("dve", lambda e: e.tensor_scalar(out=hg[:], in0=modT[:, mbase + 2], scalar1=0.5, scalar2=None, op0=ALU.mult), reads=[modT], writes=[hg])
    if post_mod:
        geff2 = p.alloc_sb("geff2", [128, 8, 2], F32)
        mk_geff(geff2, 1, 4)
    Win = p.alloc_sb("Win", [128, 8, 2 * DFF], BF16)
    Wout = p.alloc_sb("Wout", [128, NJ, 1024], BF16)
    si = 0
    for kc in range(8):
        for q4 in range(4):
            load_cast(p, lambda kc=kc, q4=q4: Win[:, kc, q4 * 1408:(q4 + 1) * 1408],
                      lambda kc=kc, q4=q4: io["ffn_in"][kc * 128:(kc + 1) * 128, q4 * 1408:(q4 + 1) * 1408], 1408, stage, si, io["ffn_in"], Win)
            si += 1
    for j in range(NJ):
        load_cast(p, lambda j=j: Wout[:, j, :], lambda j=j: io["ffn_out"][j * 128:(j + 1) * 128, :], 1024, stage, si, io["ffn_out"], Wout)
        si += 1
    NB = 256
    xin = p.alloc_sb("xin", [128, 8, NB], F32)
    xm = p.alloc_sb("xm", [128, 8, NB], BF16)
    gT = p.alloc_sb("gT", [128, NJ, NB], BF16)
    sq = [p.alloc_sb("sq%d" % i, [128, NB], F32) for i in range(2)]
    tmp = [p.alloc_sb("tmp%d" % i, [128, NB], F32) for i in range(2)]
    sil = [p.alloc_sb("sil%d" % i, [128, NB], F32) for i in range(2)]
    rstd = p.alloc_sb("rstd", [128, NB], F32)
    if pre_mix:
        pm = [p.alloc_sb("pm%d" % i, [128, 8, NB], F32) for i in range(2 if "P1" in io else 1)]
    if post_mod:
        xm2 = p.alloc_sb("xm2", [128, 8, NB], BF16)
    xTv = io["xT"].t.rearrange("(k p) t -> p k t", p=128)
    yTv = io["yT"].t.rearrange("(k p) t -> p k t", p=128)

    def rms_mod(src, dst, ge, sh_m, n, s):
        bs = banks[6]
        for kc in range(8):
            t = sq[kc % 2]
            p.op("act", lambda e, t=t, kc=kc: e.activation(out=t[:, :n], in_=src[:, kc, :n], func=AF.Square), reads=[src], writes=[t])
            p.op("pe", lambda e, t=t, kc=kc: e.matmul(bs[:, :n], K["ones"][:], t[:, :n], start=(kc == 0), stop=(kc == 7)), reads=[K["ones"], t], writes=[bs], skip_same=True)
        p.op("act", lambda e: e.activation(out=rstd[:, :n], in_=bs[:, :n], func=AF.Sqrt, scale=1.0 / 1024, bias=1e-6), reads=[bs], writes=[rstd])
        p.op("dve", lambda e: e.reciprocal(out=rstd[:, :n], in_=rstd[:, :n]), reads=[rstd], writes=[rstd])
        for kc in range(8):
            t = tmp[kc % 2]
            p.op("dve", lambda e, t=t, kc=kc: e.tensor_tensor(out=t[:, :n], in0=src[:, kc, :n], in1=rstd[:, :n], op=ALU.mult), reads=[src, rstd], writes=[t])
            p.op("act", lambda e, t=t, kc=kc: e.activation(out=dst[:, kc, :n], in_=t[:, :n], func=AF.Identity, scale=ge[:, kc, s:s + 1], bias=modT[:, sh_m, kc, s:s + 1]),
                 reads=[t, ge, modT], writes=[dst])

    for (t0, n, s) in blocks:
        p.dma("sp", lambda e, t0=t0, n=n: e.dma_start(out=xin[:, :, :n], in_=xTv[:, :, t0:t0 + n]), reads=[io["xT"]], writes=[xin])
        if pre_mix:
            pnames = ("P0", "P1") if "P1" in io else ("P0",)
            for i, nm in enumerate(pnames):
                pv = io[nm].t.rearrange("(k p) t -> p k t", p=128)
                p.dma("sp", lambda e, i=i, pv=pv, t0=t0, n=n: e.dma_start(out=pm[i][:, :, :n], in_=pv[:, :, t0:t0 + n]), reads=[io[nm]], writes=[pm[i]])
            if len(pnames) == 2:
                p.op("pool", lambda e, n=n: e.tensor_tensor(out=pm[0][:, :, :n], in0=pm[0][:, :, :n], in1=pm[1][:, :, :n], op=ALU.add), reads=[pm[0], pm[1]], writes=[pm[0]])
            for kc in range(8):
                p.op("dve", lambda e, kc=kc, n=n, s=s: e.scalar_tensor_tensor(out=xin[:, kc, :n], in0=pm[0][:, kc, :n], scalar=modT[:, 5, kc, s:s + 1], in1=xin[:, kc, :n], op0=ALU.mult, op1=ALU.add),
                     reads=[pm[0], modT, xin], writes=[xin])
        rms_mod(xin, xm, geff, mbase, n, s)
        for j in range(NJ):
            ba, bb = banks[(j % 2) * 2], banks[(j % 2) * 2 + 1]
            for (bk, off) in ((ba, 0), (bb, DFF)):
                for kc in range(8):
                    p.op("pe", lambda e, bk=bk, off=off, kc=kc, j=j, n=n: e.matmul(bk[:, :n], Win[:, kc, off + j * 128:off + (j + 1) * 128], xm[:, kc, :n], start=(kc == 0), stop=(kc == 7)),
                         reads=[Win, xm], writes=[bk], skip_same=True)
            sl = sil[j % 2]
            p.op("act", lambda e, sl=sl, ba=ba, n=n: e.activation(out=sl[:, :n], in_=ba[:, :n], func=AF.Silu), reads=[ba], writes=[sl])
            p.op("dve", lambda e, sl=sl, bb=bb, j=j, n=n: e.tensor_tensor(out=gT[:, j, :n], in0=sl[:, :n], in1=bb[:, :n], op=ALU.mult), reads=[sl, bb], writes=[gT])
        for dc in range(8):
            bo = banks[4 + dc % 2]
            for j in range(NJ):
                p.op("pe", lambda e, bo=bo, dc=dc, j=j, n=n: e.matmul(bo[:, :n], Wout[:, j, dc * 128:(dc + 1) * 128], gT[:, j, :n], start=(j == 0), stop=(j == NJ - 1)),
                     reads=[Wout, gT], writes=[bo], skip_same=True)
            p.op("dve", lambda e, bo=bo, dc=dc, n=n, s=s: e.scalar_tensor_tensor(out=xin[:, dc, :n], in0=bo[:, :n], scalar=hg[:, dc, s:s + 1], in1=xin[:, dc, :n], op0=ALU.mult, op1=ALU.add),
                 reads=[bo, hg, xin], writes=[xin])
        p.dma("sp", lambda e, t0=t0, n=n: e.dma_start(out=yTv[:, :, t0:t0 + n], in_=xin[:, :, :n]), reads=[xin], writes=[io["yT"]])
        if post_mod:
            rms_mod(xin, xm2, geff2, 3, n, s)
            xmv = io["xmT"].t.rearrange("(k p) t -> p k t", p=128)
            p.dma("sp", lambda e, t0=t0, n=n, xmv=xmv: e.dma_start(out=xmv[:, :, t0:t0 + n], in_=xm2[:, :, :n]), reads=[xm2], writes=[io["xmT"]])

import math

T = 4352
TP = 4356
BLOCKS = [(0, 256)] + [(256 + 512 * i, 512) for i in range(8)]


def pcol(t0):
    return t0 + 1 if t0 < 256 else t0 + 3

WCOLS = {}
_c = 0
for nm, n in [("gq", 64), ("gk", 64), ("gv", 128), ("gg", 128), ("glr", 16),
              ("sz0", 128), ("sx0", 128), ("sB0", 128), ("sC0", 128), ("sdt0", 128),
              ("sz1", 128), ("sx1", 128), ("sB1", 128), ("sC1", 128), ("sdt1", 128),
              ("rr", 128), ("rk", 128), ("rv", 128), ("rlw", 64), ("rla", 64), ("rlg", 128),
              ("aq", 128), ("ak", 128), ("av", 64)]:
    WCOLS[nm] = (_c, n)
    _c += n
NW = _c

PRM = {}
_c = 0
for nm in ["gb0", "gb1", "gnorm",
           "cw0_sx0", "cw1_sx0", "cw2_sx0", "cb_sx0", "cw0_sB0", "cw1_sB0", "cw2_sB0", "cb_sB0", "cw0_sC0", "cw1_sC0", "cw2_sC0", "cb_sC0",
           "cw0_sx1", "cw1_sx1", "cw2_sx1", "cb_sx1", "cw0_sB1", "cw1_sB1", "cw2_sB1", "cb_sB1", "cw0_sC1", "cw1_sC1", "cw2_sC1", "cb_sC1",
           "dtb0_0", "dtb1_0", "alog0_0", "alog1_0", "D_0", "dtb0_1", "dtb1_1", "alog0_1", "alog1_1", "D_1", "snorm", "snorm_o",
           "mu0_rr", "mu1_rr", "mu0_rk", "mu1_rk", "mu0_rv", "mu1_rv", "mu0_rlw", "mu1_rlw", "mu0_rla", "mu1_rla", "mu0_rlg", "mu1_rlg",
           "w0_0", "w0_1", "a0", "k_k", "k_a", "r_k", "ln_g", "ln_b", "qn", "kn"]:
    PRM[nm] = _c
    _c += 1
NPRM = _c


class K2:
    pass


def load_prm(S):
    p = S.p
    src = S.io["prm"]
    p.dma("sp", lambda e: e.dma_start(out=S.prm[:], in_=src[:]), reads=[src], writes=[S.prm])


def k2_setup(p, io, banks=None):
    S = K2()
    S.p = p
    S.io = io
    S.ybase = 0
    S.banks = banks if banks is not None else [p.alloc_ps("bank%d" % i, [128, 512], F32) for i in range(8)]
    S.prm = p.alloc_sb("prm", [128, NPRM], F32)
    load_prm(S)
    S.ones = p.alloc_sb("ones", [128, 128], F32)
    p.op("pool", lambda e: e.memset(S.ones[:], 1.0), writes=[S.ones])
    S.bm = p.alloc_sb("bm", [128, 128], F32)
    p.op("pool", lambda e: e.memset(S.bm[:], 0.0), writes=[S.bm])
    p.op("pool", lambda e: e.memset(S.bm[0:64, 0:64], 1.0), reads=[S.bm], writes=[S.bm])
    p.op("pool", lambda e: e.memset(S.bm[64:128, 64:128], 1.0), reads=[S.bm], writes=[S.bm])
    S.bmg = p.alloc_sb("bmg", [128, 64], F32)
    p.op("pool", lambda e: e.memset(S.bmg[:], 0.0), writes=[S.bmg])
    p.op("pool", lambda e: e.memset(S.bmg[0:64, 0:32], 1.0), reads=[S.bmg], writes=[S.bmg])
    p.op("pool", lambda e: e.memset(S.bmg[64:128, 32:64], 1.0), reads=[S.bmg], writes=[S.bmg])
    S.I2 = p.alloc_sb("I2", [128, 64], F32)
    p.op("pool", lambda e: e.memset(S.I2[:], 1.0), writes=[S.I2])
    for hb in range(2):
        p.op("pool", lambda e, hb=hb: e.affine_select(out=S.I2[hb * 64:(hb + 1) * 64, :], in_=S.I2[hb * 64:(hb + 1) * 64, :], pattern=[[-1, 64]],
                                                      compare_op=ALU.is_equal, fill=0.0, base=0, channel_multiplier=1), reads=[S.I2], writes=[S.I2])
    S.ident = p.alloc_sb("ident", [128, 128], F32)
    p.op("pool", lambda e: e.memset(S.ident[:], 1.0), writes=[S.ident])
    p.op("pool", lambda e: e.affine_select(out=S.ident[:], in_=S.ident[:], pattern=[[-1, 128]], compare_op=ALU.is_equal, fill=0.0, base=0, channel_multiplier=1),
         reads=[S.ident], writes=[S.ident])
    S.stage = p.alloc_sb("stage", [128, 1024], F32)
    S.Wb = p.alloc_sb("Wb", [128, 8, 640], BF16)
    S.xmb = p.alloc_sb("xmb", [128, 8, 512], BF16)
    S.arr = [p.alloc_sb("arr%d" % i, [128, TP], F32) for i in range(8)]
    S.Vd = [p.alloc_sb("Vd%d" % i, [128, 4, 64], F32) for i in range(4)]
    S.Y0 = [p.alloc_sb("Y0%d" % i, [128, 4, 64], F32) for i in range(4)]
    S.Abd = [p.alloc_sb("Abd%d" % i, [128, 4, 128], F32) for i in range(4)]
    S.ST = [p.alloc_sb("ST%d" % i, [128, 64], F32) for i in range(8)]
    S.Z = [p.alloc_sb("Z%d" % i, [128, 64], F32) for i in range(2)]
    S.t512 = [p.alloc_sb("t512_%d" % i, [128, 512], F32) for i in range(3)]
    S.yb = p.alloc_sb("yb", [128, 512], BF16)
    return S


def prm(S, nm, np_=128):
    c = PRM[nm]
    return S.prm[0:np_, c:c + 1]


def load_w(S, names):
    p = S.p
    wsrc = S.io["wsel"]
    offs = {}
    o = 0
    for nm in names:
        c0, n = WCOLS[nm]
        offs[nm] = (o, n)
        for kc in range(8):
            p.dma("sp", lambda e, kc=kc, c0=c0, n=n: e.dma_start(out=S.stage[:, 0:n], in_=wsrc[kc * 128:(kc + 1) * 128, c0:c0 + n]), reads=[wsrc], writes=[S.stage])
            p.op("pool", lambda e, kc=kc, o=o, n=n: e.tensor_copy(out=S.Wb[:, kc, o:o + n], in_=S.stage[:, 0:n]), reads=[S.stage], writes=[S.Wb])
        o += n
    assert o <= 640
    return offs


def proj_fm(S, offs, dsts):
    p = S.p
    xv = S.io["xmT"].t.rearrange("(k p) t -> p k t", p=128)
    bi = 0
    for (t0, n) in BLOCKS:
        p.dma("sp", lambda e, t0=t0, n=n: e.dma_start(out=S.xmb[:, :, :n], in_=xv[:, :, t0:t0 + n]), reads=[S.io["xmT"]], writes=[S.xmb])
        for (nm, dst, padded) in dsts:
            o, m = offs[nm]
            bk = S.banks[bi % 2]
            bi += 1
            for kc in range(8):
                p.op("pe", lambda e, bk=bk, kc=kc, o=o, m=m, n=n: e.matmul(bk[0:m, :n], S.Wb[:, kc, o:o + m], S.xmb[:, kc, :n], start=(kc == 0), stop=(kc == 7)),
                     reads=[S.Wb, S.xmb], writes=[bk], skip_same=True)
            c = pcol(t0) if padded else t0
            p.op("act", lambda e, bk=bk, m=m, n=n, c=c, dst=dst: e.activation(out=dst[0:m, c:c + n], in_=bk[0:m, :n], func=AF.Copy), reads=[bk], writes=[dst])


def zero_pads(S, buf):
    p = S.p
    for c in (0, 257, 258, 4355):
        p.op("pool", lambda e, c=c: e.memset(buf[:, c:c + 1], 0.0), reads=[buf], writes=[buf])


def conv3(S, src, dst, np_, w0, w1, w2, bias=None):
    p = S.p
    for (i, o, n) in ((0, 0, 256), (258, 256, 4096)):
        if bias is not None:
            p.op("dve", lambda e, i=i, o=o, n=n: e.tensor_scalar(out=dst[0:np_, o:o + n], in0=src[0:np_, i + 1:i + 1 + n], scalar1=w1, scalar2=bias, op0=ALU.mult, op1=ALU.add),
                 reads=[src, S.prm], writes=[dst])
        else:
            p.op("dve", lambda e, i=i, o=o, n=n: e.tensor_scalar(out=dst[0:np_, o:o + n], in0=src[0:np_, i + 1:i + 1 + n], scalar1=w1, scalar2=None, op0=ALU.mult),
                 reads=[src, S.prm], writes=[dst])
        p.op("dve", lambda e, i=i, o=o, n=n: e.scalar_tensor_tensor(out=dst[0:np_, o:o + n], in0=src[0:np_, i:i + n], scalar=w0, in1=dst[0:np_, o:o + n], op0=ALU.mult, op1=ALU.add),
             reads=[src, dst, S.prm], writes=[dst])
        p.op("dve", lambda e, i=i, o=o, n=n: e.scalar_tensor_tensor(out=dst[0:np_, o:o + n], in0=src[0:np_, i + 2:i + 2 + n], scalar=w2, in1=dst[0:np_, o:o + n], op0=ALU.mult, op1=ALU.add),
             reads=[src, dst, S.prm], writes=[dst])


def scan_gen(S, ci, NP, w, k, v, r, heads, ab, bmV, dirn, yacc, ybank, vbank, sabank, yadd, vmul=None):
    p = S.p
    TB = 4
    ST = [S.ST[ci * 4 + i] for i in range(4)]
    Vd = [S.Vd[ci * 2], S.Vd[ci * 2 + 1]]
    Y0 = [S.Y0[ci * 2], S.Y0[ci * 2 + 1]]
    Z = S.Z[ci]
    Abd = [S.Abd[ci * 2], S.Abd[ci * 2 + 1]]
    p.op("pool", lambda e: e.memset(ST[0][:], 0.0), reads=[ST[0]], writes=[ST[0]])
    cur = 0
    if dirn == 0:
        blocks = BLOCKS
    else:
        blocks = [BLOCKS[0]] + BLOCKS[:0:-1]
    blist = []
    for (t0, n) in blocks:
        batches = list(range(t0, t0 + n, TB))
        if dirn == 1:
            batches = batches[::-1]
        for bi_, tb0 in enumerate(batches):
            blist.append((t0, n, tb0, bi_ == len(batches) - 1))

    def prep(nb):
        t0, n, tb0, last = blist[nb]
        vd, y0 = Vd[nb % 2], Y0[nb % 2]
        abd = Abd[nb % 2]
        p.op("pool", lambda e: e.tensor_tensor(out=vd[:], in0=S.I2[:].unsqueeze(1).broadcast_to([128, TB, 64]),
                                               in1=v[:, tb0:tb0 + TB].unsqueeze(2).broadcast_to([128, TB, 64]), op=ALU.mult), reads=[S.I2, v], writes=[vd])
        if vmul is not None:
            p.op("pool", lambda e: e.tensor_tensor(out=vd[:], in0=vd[:], in1=vmul[:, tb0:tb0 + TB].unsqueeze(2).broadcast_to([128, TB, 64]), op=ALU.mult), reads=[vd, vmul], writes=[vd])
        p.op("pe", lambda e: e.matmul(vbank[0:NP, 0:TB * 64], bmV[:, 0:NP], vd[:].rearrange("p a b -> p (a b)"), start=True, stop=True), reads=[bmV, vd], writes=[vbank], skip_same=True)
        p.op("dve", lambda e: e.tensor_tensor(out=y0[0:NP], in0=vbank[0:NP, 0:TB * 64].rearrange("p (a b) -> p a b", b=64),
                                              in1=k[0:NP, tb0:tb0 + TB].unsqueeze(2).broadcast_to([NP, TB, 64]), op=ALU.mult), reads=[vbank, k], writes=[y0])
        if ab is not None:
            p.op("pool", lambda e: e.tensor_tensor(out=abd[:], in0=S.bm[:].unsqueeze(1).broadcast_to([128, TB, 128]),
                                                   in1=ab[0][:, tb0:tb0 + TB].unsqueeze(2).broadcast_to([128, TB, 128]), op=ALU.mult), reads=[S.bm, ab[0]], writes=[abd])

    pend = []

    def flush_y():
        while pend:
            sb_, t_, col_ = pend.pop(0)
            for (kr, vr) in heads:
                p.op("pe", lambda e, sb_=sb_, kr=kr, vr=vr, col_=col_, t_=t_: e.matmul(ybank[vr[0]:vr[1], col_:col_ + 1], sb_[kr[0]:kr[1], :], r[kr[0]:kr[1], t_:t_ + 1], start=True, stop=True),
                     reads=[sb_, r], writes=[ybank], skip_same=True)

    prep(0)
    for nb in range(len(blist)):
        t0, n, tb0, last = blist[nb]
        y0 = Y0[nb % 2]
        abd = Abd[nb % 2]
        steps = list(range(tb0, tb0 + TB))
        if dirn == 1:
            steps = steps[::-1]
        for si, t in enumerate(steps):
            if si == 1 and nb + 1 < len(blist):
                prep(nb + 1)
            tt = t - tb0
            s_in, s_out = ST[cur], ST[(cur + 1) % 4]
            cur = (cur + 1) % 4
            if ab is not None:
                p.op("dve", lambda e, s_in=s_in, t=t, tt=tt, y0=y0: e.scalar_tensor_tensor(out=Z[0:NP], in0=s_in[0:NP], scalar=w[0:NP, t:t + 1], in1=y0[0:NP, tt, :], op0=ALU.mult, op1=ALU.add),
                     reads=[s_in, w, y0], writes=[Z])
                p.op("pe", lambda e, s_in=s_in, tt=tt, abd=abd: e.matmul(sabank[0:NP, 0:64], abd[:, tt, :], s_in[:], start=True, stop=True), reads=[abd, s_in], writes=[sabank], skip_same=True)
                flush_y()
                p.op("dve", lambda e, s_out=s_out, t=t: e.scalar_tensor_tensor(out=s_out[0:NP], in0=sabank[0:NP, 0:64], scalar=ab[1][0:NP, t:t + 1], in1=Z[0:NP], op0=ALU.mult, op1=ALU.add),
                     reads=[sabank, ab[1], Z], writes=[s_out])
            else:
                p.op("dve", lambda e, s_in=s_in, s_out=s_out, t=t, tt=tt, y0=y0: e.scalar_tensor_tensor(out=s_out[0:NP], in0=s_in[0:NP], scalar=w[0:NP, t:t + 1], in1=y0[0:NP, tt, :], op0=ALU.mult, op1=ALU.add),
                     reads=[s_in, w, y0], writes=[s_out])
                flush_y()
            pend.append((s_out, t, t - t0))
            yield
        if last:
            flush_y()
            if yadd:
                p.op("dve", lambda e, t0=t0, n=n: e.tensor_tensor(out=yacc[:, t0:t0 + n], in0=ybank[:, 0:n], in1=yacc[:, t0:t0 + n], op=ALU.add), reads=[ybank, yacc], writes=[yacc])
            else:
                p.op("act", lambda e, t0=t0, n=n: e.activation(out=yacc[:, t0:t0 + n], in_=ybank[:, 0:n], func=AF.Copy), reads=[ybank], writes=[yacc])


def run_gens(gens):
    alive = list(gens)
    while alive:
        for g in list(alive):
            try:
                next(g)
            except StopIteration:
                alive.remove(g)


def gla(S):
    p = S.p
    A = S.arr
    q, k, lr, wd, v, y = A[0], A[1], A[2], A[3], A[4], A[5]
    wdb = A[6]
    offs = load_w(S, ["gq", "gk", "gv", "glr"])
    proj_fm(S, offs, [("gq", q, False), ("gk", k, False), ("gv", v, False), ("glr", lr, False)])
    p.op("dve", lambda e: e.tensor_scalar(out=q[0:64, 0:T], in0=q[0:64, 0:T], scalar1=32 ** -0.5, scalar2=None, op0=ALU.mult), reads=[q], writes=[q])
    gw = p.alloc_sb("gwdec", [16, 2, 64], F32)
    gsrc = S.io["gwdec"]
    p.dma("sp", lambda e: e.dma_start(out=gw[:], in_=gsrc[:]), reads=[gsrc], writes=[gw])
    for d, dst in ((0, wd), (1, wdb)):
        for bi, (t0, n) in enumerate(BLOCKS):
            bk = S.banks[bi % 2]
            p.op("pe", lambda e, bk=bk, d=d, t0=t0, n=n: e.matmul(bk[0:64, :n], gw[:, d, :], lr[0:16, t0:t0 + n], start=True, stop=True), reads=[gw, lr], writes=[bk], skip_same=True)
            p.op("act", lambda e, bk=bk, d=d, t0=t0, n=n, dst=dst: e.activation(out=dst[0:64, t0:t0 + n], in_=bk[0:64, :n], func=AF.Exp, scale=-1.0, bias=None) if False else
                 e.activation(out=dst[0:64, t0:t0 + n], in_=bk[0:64, :n], func=AF.Identity, bias=prm(S, "gb%d" % d, 64), scale=1.0), reads=[bk, S.prm], writes=[dst])
        p.op("act", lambda e, dst=dst: e.activation(out=dst[0:64, 0:T], in_=dst[0:64, 0:T], func=AF.Exp, scale=-1.0), reads=[dst], writes=[dst])
        p.op("act", lambda e, dst=dst: e.activation(out=dst[0:64, 0:T], in_=dst[0:64, 0:T], func=AF.Ln, bias=1.0), reads=[dst], writes=[dst])
        p.op("act", lambda e, dst=dst: e.activation(out=dst[0:64, 0:T], in_=dst[0:64, 0:T], func=AF.Exp, scale=-1.0 / 16), reads=[dst], writes=[dst])
    heads = [((0, 32), (0, 64)), ((32, 64), (64, 128))]
    p.op("pool", lambda e: e.memset(y[:, 0:T], 0.0), reads=[y], writes=[y])
    yb0, yb1, vb0, vb1 = S.banks[2], S.banks[3], S.banks[4], S.banks[5]
    g0 = scan_gen(S, 0, 64, wd, k, v, q, heads, None, S.bmg, 0, y, yb0, vb0, None, True)
    g1 = scan_gen(S, 1, 64, wdb, k, v, q, heads, None, S.bmg, 1, y, yb1, vb1, None, True)
    run_gens([g0, g1])
    gg = A[0]
    offs = load_w(S, ["gg"])
    proj_fm(S, offs, [("gg", gg, False)])
    finish_rms_gate(S, y, gg, "gnorm", 0, 1.0 / 64, S.bm)


def finish_rms_gate(S, y, g, normname, m, inv_n, bmat, extra_ssq=None):
    p = S.p
    t1, t2, t3 = S.t512
    yv = S.io["yT"].t
    YB = S.ybase
    for bi, (t0, n) in enumerate(BLOCKS):
        bk = S.banks[bi % 2]
        p.op("act", lambda e, t0=t0, n=n: e.activation(out=t1[:, :n], in_=y[:, t0:t0 + n], func=AF.Square), reads=[y], writes=[t1])
        p.op("pe", lambda e, bk=bk, n=n: e.matmul(bk[:, :n], bmat[:], t1[:, :n], start=True, stop=True), reads=[bmat, t1], writes=[bk], skip_same=True)
        p.op("act", lambda e, bk=bk, n=n: e.activation(out=t2[:, :n], in_=bk[:, :n], func=AF.Sqrt, scale=inv_n, bias=1e-6), reads=[bk], writes=[t2])
        p.op("dve", lambda e, n=n: e.reciprocal(out=t2[:, :n], in_=t2[:, :n]), reads=[t2], writes=[t2])
        p.op("act", lambda e, t0=t0, n=n: e.activation(out=t3[:, :n], in_=g[:, t0:t0 + n], func=AF.Silu), reads=[g], writes=[t3])
        p.op("dve", lambda e, t0=t0, n=n: e.scalar_tensor_tensor(out=t2[:, :n], in0=t2[:, :n], scalar=prm(S, normname), in1=y[:, t0:t0 + n], op0=ALU.mult, op1=ALU.mult), reads=[t2, S.prm, y], writes=[t2])
        p.op("dve", lambda e, n=n: e.tensor_tensor(out=S.yb[:, :n], in0=t2[:, :n], in1=t3[:, :n], op=ALU.mult), reads=[t2, t3], writes=[S.yb])
        p.dma("sp", lambda e, t0=t0, n=n: e.dma_start(out=yv[YB + m, :, t0:t0 + n], in_=S.yb[:, :n]), reads=[S.yb], writes=[S.io["yT"]])


def der_setup(S):
    p = S.p
    S.der = p.alloc_sb("der", [128, 24], F32)
    S.dn = {}

    def col(nm):
        if nm not in S.dn:
            S.dn[nm] = len(S.dn)
        c = S.dn[nm]
        return S.der[:, c:c + 1]
    S.dcol = col


def ssd_group(S, slot, own, ssq_o):
    p = S.p
    A = S.arr
    P, xs, Bk, Cr, dtr, wd, dtb, wdb = A[0], A[1], A[2], A[3], A[4], A[5], A[6], A[7]
    g = str(slot)
    offs = load_w(S, ["sx" + g, "sB" + g, "sC" + g, "sdt" + g])
    zero_pads(S, P)
    for nm, dst in (("sx" + g, xs), ("sB" + g, Bk), ("sC" + g, Cr)):
        proj_fm(S, offs, [(nm, P, True)])
        conv3(S, P, dst, 128, prm(S, "cw0_" + nm), prm(S, "cw1_" + nm), prm(S, "cw2_" + nm), prm(S, "cb_" + nm))
        p.op("act", lambda e, dst=dst: e.activation(out=dst[:, 0:T], in_=dst[:, 0:T], func=AF.Silu), reads=[dst], writes=[dst])
    proj_fm(S, offs, [("sdt" + g, dtr, False)])
    y = P
    p.op("pool", lambda e: e.memset(y[:, 0:T], 0.0), reads=[y], writes=[y])
    heads = [((0, 64), (0, 64)), ((64, 128), (64, 128))]
    for d, dts, wds_ in ((1, dtb, wdb), (0, dtr, wd)):
        na = S.dcol("na%d_%s" % (d, g))
        p.op("act", lambda e, d=d, na=na: e.activation(out=na, in_=prm(S, "alog%d_%s" % (d, g)), func=AF.Exp), reads=[S.prm], writes=[S.der])
        p.op("dve", lambda e, na=na: e.tensor_scalar(out=na, in0=na, scalar1=-1.0, scalar2=None, op0=ALU.mult), reads=[S.der], writes=[S.der])
        p.op("act", lambda e, d=d, dts=dts: e.activation(out=dts[:, 0:T], in_=dtr[:, 0:T], func=AF.Exp, bias=prm(S, "dtb%d_%s" % (d, g)), scale=1.0), reads=[dtr, S.prm], writes=[dts])
        p.op("act", lambda e, dts=dts: e.activation(out=dts[:, 0:T], in_=dts[:, 0:T], func=AF.Ln, bias=1.0), reads=[dts], writes=[dts])
        p.op("act", lambda e, na=na, dts=dts, wds_=wds_: e.activation(out=wds_[:, 0:T], in_=dts[:, 0:T], func=AF.Exp, scale=na), reads=[dts, S.der], writes=[wds_])
    g0 = scan_gen(S, 0, 128, wd, Bk, xs, Cr, heads, None, S.bm, 0, y, S.banks[2], S.banks[4], None, True, vmul=dtr)
    g1 = scan_gen(S, 1, 128, wdb, Bk, xs, Cr, heads, None, S.bm, 1, y, S.banks[3], S.banks[6], None, True, vmul=dtb)
    run_gens([g0, g1])
    z = wd
    offs = load_w(S, ["sz" + g])
    proj_fm(S, offs, [("sz" + g, z, False)])
    p.op("dve", lambda e: e.scalar_tensor_tensor(out=y[:, 0:T], in0=xs[:, 0:T], scalar=prm(S, "D_" + g), in1=y[:, 0:T], op0=ALU.mult, op1=ALU.add), reads=[xs, S.prm, y], writes=[y])
    p.op("act", lambda e: e.activation(out=z[:, 0:T], in_=z[:, 0:T], func=AF.Silu), reads=[z], writes=[z])
    p.op("dve", lambda e: e.tensor_tensor(out=y[:, 0:T], in0=y[:, 0:T], in1=z[:, 0:T], op=ALU.mult), reads=[y, z], writes=[y])
    t1, t2, t3 = S.t512
    yv = S.io["yT"].t
    YB = S.ybase
    sq0, yg0 = S.io["ssq0"], S.io["yg0"]
    for bi, (t0, n) in enumerate(BLOCKS):
        bk = S.banks[bi % 2]
        p.op("act", lambda e, t0=t0, n=n: e.activation(out=t1[:, :n], in_=y[:, t0:t0 + n], func=AF.Square), reads=[y], writes=[t1])
        p.op("pe", lambda e, bk=bk, n=n: e.matmul(bk[:, :n], S.ones[:], t1[:, :n], start=True, stop=True), reads=[S.ones, t1], writes=[bk], skip_same=True)
        if not own:
            p.op("act", lambda e, bk=bk, n=n: e.activation(out=t2[:, :n], in_=bk[:, :n], func=AF.Copy), reads=[bk], writes=[t2])
            p.dma("sp", lambda e, t0=t0, n=n: e.dma_start(out=sq0[:, t0:t0 + n], in_=t2[:, :n]), reads=[t2], writes=[sq0])
            p.dma("sp", lambda e, t0=t0, n=n: e.dma_start(out=yg0[:, t0:t0 + n], in_=y[:, t0:t0 + n]), reads=[y], writes=[yg0])
        else:
            p.dma("sp", lambda e, t0=t0, n=n: e.dma_start(out=t3[:, :n], in_=sq0[:, t0:t0 + n]), reads=[sq0], writes=[t3])
            p.op("dve", lambda e, bk=bk, n=n: e.tensor_tensor(out=t2[:, :n], in0=bk[:, :n], in1=t3[:, :n], op=ALU.add), reads=[bk, t3], writes=[t2])
            p.op("act", lambda e, n=n: e.activation(out=t2[:, :n], in_=t2[:, :n], func=AF.Sqrt, scale=1.0 / 256, bias=1e-6), reads=[t2], writes=[t2])
            p.op("dve", lambda e, n=n: e.reciprocal(out=t2[:, :n], in_=t2[:, :n]), reads=[t2], writes=[t2])
            p.op("dve", lambda e, t0=t0, n=n: e.scalar_tensor_tensor(out=S.yb[:, :n], in0=t2[:, :n], scalar=prm(S, "snorm"), in1=y[:, t0:t0 + n], op0=ALU.mult, op1=ALU.mult), reads=[t2, S.prm, y], writes=[S.yb])
            p.dma("sp", lambda e, t0=t0, n=n: e.dma_start(out=yv[YB + 1, :, t0:t0 + n], in_=S.yb[:, :n]), reads=[S.yb], writes=[S.io["yT"]])
            if S.ssd_both:
                p.dma("sp", lambda e, t0=t0, n=n: e.dma_start(out=t3[:, :n], in_=yg0[:, t0:t0 + n]), reads=[yg0], writes=[t3])
                p.op("dve", lambda e, n=n: e.scalar_tensor_tensor(out=S.yb[:, :n], in0=t2[:, :n], scalar=prm(S, "snorm_o"), in1=t3[:, :n], op0=ALU.mult, op1=ALU.mult), reads=[t2, S.prm, t3], writes=[S.yb])
                p.dma("sp", lambda e, t0=t0, n=n: e.dma_start(out=yv[YB - 4 + 1, :, t0:t0 + n], in_=S.yb[:, :n]), reads=[S.yb], writes=[S.io["yT"]])


def ssd(S):
    ssd_group(S, 0, False, None)
    ssd_group(S, 1, True, None)


def rwkv(S):
    p = S.p
    A = S.arr
    P, r, k, v, A4, A5, A6, A7 = A
    rw = p.alloc_sb("rww", [128, 4, 128], F32)
    rsrc = S.io["rww"]
    p.dma("sp", lambda e: e.dma_start(out=rw[:], in_=rsrc[:]), reads=[rsrc], writes=[rw])

    def shift(nm, dst, np_):
        w1 = S.dcol("w1_" + nm)
        p.op("dve", lambda e: e.tensor_scalar(out=w1, in0=prm(S, "mu0_" + nm), scalar1=-1.0, scalar2=1.0, op0=ALU.mult, op1=ALU.add), reads=[S.prm], writes=[S.der])
        p.op("dve", lambda e: e.tensor_tensor(out=w1, in0=w1, in1=prm(S, "mu1_" + nm), op=ALU.subtract), reads=[S.prm, S.der], writes=[S.der])
        conv3(S, P, dst, np_, prm(S, "mu0_" + nm, np_), w1[0:np_], prm(S, "mu1_" + nm, np_))

    offs = load_w(S, ["rr", "rk", "rv", "rla", "rlw"])
    zero_pads(S, P)
    for nm, dst, np_ in (("rr", r, 128), ("rk", k, 128), ("rv", v, 128), ("rla", A4, 64)):
        proj_fm(S, offs, [(nm, P, True)])
        shift(nm, dst, np_)
    for bi, (t0, n) in enumerate(BLOCKS):
        bk = S.banks[bi % 2]
        p.op("pe", lambda e, bk=bk, t0=t0, n=n: e.matmul(bk[:, :n], rw[0:64, 2, :], A4[0:64, t0:t0 + n], start=True, stop=True), reads=[rw, A4], writes=[bk], skip_same=True)
        p.op("act", lambda e, bk=bk, t0=t0, n=n: e.activation(out=A5[:, t0:t0 + n], in_=bk[:, :n], func=AF.Sigmoid, bias=prm(S, "a0"), scale=1.0), reads=[bk, S.prm], writes=[A5])
    p.op("dve", lambda e: e.tensor_scalar(out=A6[:, 0:T], in0=k[:, 0:T], scalar1=prm(S, "k_k"), scalar2=None, op0=ALU.mult), reads=[k, S.prm], writes=[A6])
    t1, t2, t3 = S.t512
    for bi, (t0, n) in enumerate(BLOCKS):
        bk = S.banks[bi % 2]
        p.op("act", lambda e, t0=t0, n=n: e.activation(out=t1[:, :n], in_=A6[:, t0:t0 + n], func=AF.Square), reads=[A6], writes=[t1])
        p.op("pe", lambda e, bk=bk, n=n: e.matmul(bk[:, :n], S.bm[:], t1[:, :n], start=True, stop=True), reads=[S.bm, t1], writes=[bk], skip_same=True)
        p.op("act", lambda e, bk=bk, n=n: e.activation(out=t2[:, :n], in_=bk[:, :n], func=AF.Sqrt), reads=[bk], writes=[t2])
        p.op("dve", lambda e, n=n: e.tensor_scalar(out=t2[:, :n], in0=t2[:, :n], scalar1=1e-12, scalar2=None, op0=ALU.max), reads=[t2], writes=[t2])
        p.op("dve", lambda e, n=n: e.reciprocal(out=t2[:, :n], in_=t2[:, :n]), reads=[t2], writes=[t2])
        p.op("dve", lambda e, t0=t0, n=n: e.tensor_tensor(out=A6[:, t0:t0 + n], in0=A6[:, t0:t0 + n], in1=t2[:, :n], op=ALU.mult), reads=[A6, t2], writes=[A6])
    p.op("dve", lambda e: e.tensor_tensor(out=A4[:, 0:T], in0=A6[:, 0:T], in1=A5[:, 0:T], op=ALU.mult), reads=[A6, A5], writes=[A4])
    omka = S.dcol("omka")
    p.op("dve", lambda e: e.tensor_scalar(out=omka, in0=prm(S, "k_a"), scalar1=-1.0, scalar2=1.0, op0=ALU.mult, op1=ALU.add), reads=[S.prm], writes=[S.der])
    p.op("dve", lambda e: e.tensor_scalar(out=A5[:, 0:T], in0=A5[:, 0:T], scalar1=prm(S, "k_a"), scalar2=omka, op0=ALU.mult, op1=ALU.add), reads=[A5, S.prm, S.der], writes=[A5])
    p.op("dve", lambda e: e.tensor_tensor(out=k[:, 0:T], in0=k[:, 0:T], in1=A5[:, 0:T], op=ALU.mult), reads=[k, A5], writes=[k])
    p.op("dve", lambda e: e.tensor_scalar(out=A5[:, 0:T], in0=A6[:, 0:T], scalar1=-1.0, scalar2=None, op0=ALU.mult), reads=[A6], writes=[A5])
    proj_fm(S, offs, [("rlw", P, True)])
    shift("rlw", A7, 64)
    p.op("act", lambda e: e.activation(out=A7[0:64, 0:T], in_=A7[0:64, 0:T], func=AF.Tanh), reads=[A7], writes=[A7])
    heads = [((0, 64), (0, 64)), ((64, 128), (64, 128))]
    wds = [A6, P]
    for d in range(2):
        wdd = wds[d]
        for bi, (t0, n) in enumerate(BLOCKS):
            bk = S.banks[bi % 2]
            p.op("pe", lambda e, bk=bk, t0=t0, n=n, d=d: e.matmul(bk[:, :n], rw[0:64, d, :], A7[0:64, t0:t0 + n], start=True, stop=True), reads=[rw, A7], writes=[bk], skip_same=True)
            p.op("act", lambda e, bk=bk, t0=t0, n=n, d=d, wdd=wdd: e.activation(out=wdd[:, t0:t0 + n], in_=bk[:, :n], func=AF.Sigmoid, bias=prm(S, "w0_%d" % d), scale=1.0), reads=[bk, S.prm], writes=[wdd])
        p.op("act", lambda e, wdd=wdd: e.activation(out=wdd[:, 0:T], in_=wdd[:, 0:T], func=AF.Exp, scale=-math.exp(-0.5)), reads=[wdd], writes=[wdd])
    y = A7
    p.op("pool", lambda e: e.memset(y[:, 0:T], 0.0), reads=[y], writes=[y])
    g0 = scan_gen(S, 0, 128, wds[0], k, v, r, heads, (A5, A4), S.bm, 0, y, S.banks[2], S.banks[4], S.banks[5], True)
    g1 = scan_gen(S, 1, 128, wds[1], k, v, r, heads, (A5, A4), S.bm, 1, y, S.banks[3], S.banks[6], S.banks[7], True)
    run_gens([g0, g1])
    offs = load_w(S, ["rlg"])
    zero_pads(S, A6)
    proj_fm(S, offs, [("rlg", A6, True)])
    w1 = S.dcol("w1_rlg")
    p.op("dve", lambda e: e.tensor_scalar(out=w1, in0=prm(S, "mu0_rlg"), scalar1=-1.0, scalar2=1.0, op0=ALU.mult, op1=ALU.add), reads=[S.prm], writes=[S.der])
    p.op("dve", lambda e: e.tensor_tensor(out=w1, in0=w1, in1=prm(S, "mu1_rlg"), op=ALU.subtract), reads=[S.prm, S.der], writes=[S.der])
    conv3(S, A6, P, 128, prm(S, "mu0_rlg"), w1, prm(S, "mu1_rlg"))
    p.op("act", lambda e: e.activation(out=P[:, 0:T], in_=P[:, 0:T], func=AF.Sigmoid), reads=[P], writes=[P])
    p.op("dve", lambda e: e.scalar_tensor_tensor(out=A4[:, 0:T], in0=r[:, 0:T], scalar=prm(S, "r_k"), in1=k[:, 0:T], op0=ALU.mult, op1=ALU.mult), reads=[r, S.prm, k], writes=[A4])
    yv = S.io["yT"].t
    YB = S.ybase
    for bi, (t0, n) in enumerate(BLOCKS):
        bk, bk2, bk3, bk4 = S.banks[0], S.banks[1], S.banks[6], S.banks[7]
        p.op("pe", lambda e, t0=t0, n=n: e.matmul(bk[:, :n], S.bm[:], y[:, t0:t0 + n], start=True, stop=True), reads=[S.bm, y], writes=[bk], skip_same=True)
        p.op("dve", lambda e, t0=t0, n=n: e.scalar_tensor_tensor(out=t1[:, :n], in0=bk[:, :n], scalar=-1.0 / 64, in1=y[:, t0:t0 + n], op0=ALU.mult, op1=ALU.add), reads=[bk, y], writes=[t1])
        p.op("act", lambda e, n=n: e.activation(out=t2[:, :n], in_=t1[:, :n], func=AF.Square), reads=[t1], writes=[t2])
        p.op("pe", lambda e, n=n: e.matmul(bk2[:, :n], S.bm[:], t2[:, :n], start=True, stop=True), reads=[S.bm, t2], writes=[bk2], skip_same=True)
        p.op("act", lambda e, n=n: e.activation(out=t2[:, :n], in_=bk2[:, :n], func=AF.Sqrt, scale=1.0 / 64, bias=64e-5), reads=[bk2], writes=[t2])
        p.op("dve", lambda e, n=n: e.reciprocal(out=t2[:, :n], in_=t2[:, :n]), reads=[t2], writes=[t2])
        p.op("dve", lambda e, n=n: e.scalar_tensor_tensor(out=t1[:, :n], in0=t1[:, :n], scalar=prm(S, "ln_g"), in1=t2[:, :n], op0=ALU.mult, op1=ALU.mult), reads=[t1, S.prm, t2], writes=[t1])
        p.op("pe", lambda e, t0=t0, n=n: e.matmul(bk3[:, :n], S.bm[:], A4[:, t0:t0 + n], start=True, stop=True), reads=[S.bm, A4], writes=[bk3], skip_same=True)
        p.op("dve", lambda e, t0=t0, n=n: e.tensor_tensor(out=t2[:, :n], in0=bk3[:, :n], in1=v[:, t0:t0 + n], op=ALU.mult), reads=[bk3, v], writes=[t2])
        p.op("dve", lambda e, n=n: e.scalar_tensor_tensor(out=t1[:, :n], in0=t1[:, :n], scalar=prm(S, "ln_b"), in1=t2[:, :n], op0=ALU.add, op1=ALU.add), reads=[t1, S.prm, t2], writes=[t1])
        p.op("pe", lambda e, t0=t0, n=n: e.matmul(bk4[:, :n], rw[:, 3, :], P[:, t0:t0 + n], start=True, stop=True), reads=[rw, P], writes=[bk4], skip_same=True)
        p.op("dve", lambda e, n=n: e.tensor_tensor(out=S.yb[:, :n], in0=bk4[:, :n], in1=t1[:, :n], op=ALU.mult), reads=[bk4, t1], writes=[S.yb])
        p.dma("sp", lambda e, t0=t0, n=n: e.dma_start(out=yv[YB + 2, :, t0:t0 + n], in_=S.yb[:, :n]), reads=[S.yb], writes=[S.io["yT"]])


def attn(S):
    p = S.p
    A = S.arr
    q, k = A[0], A[1]
    V1 = p.alloc_sb("V1", [128, 34, 65], BF16)
    PT = [p.alloc_sb("PT%d" % i, [128, 512], BF16) for i in range(2)]
    perm = p.alloc_sb("perm", [128, 128], F32)
    p.dma("sp", lambda e: e.dma_start(out=perm[:], in_=S.io["perm"][:]), reads=[S.io["perm"]], writes=[perm])
    offs = load_w(S, ["aq", "ak", "av"])
    proj_fm(S, offs, [("aq", q, False), ("ak", k, False)])
    p.op("pool", lambda e: e.memset(V1[:], 1.0), writes=[V1])
    xv = S.io["xmT"].t.rearrange("(k p) t -> p k t", p=128)
    ov, nv = offs["av"]
    for (t0, n) in BLOCKS:
        p.dma("sp", lambda e, t0=t0, n=n: e.dma_start(out=S.xmb[:, :, :n], in_=xv[:, :, t0:t0 + n]), reads=[S.io["xmT"]], writes=[S.xmb])
        for s in range(n // 128):
            bk = S.banks[s % 2]
            for kc in range(8):
                p.op("pe", lambda e, bk=bk, kc=kc, s=s: e.matmul(bk[:, 0:64], S.xmb[:, kc, s * 128:(s + 1) * 128], S.Wb[:, kc, ov:ov + 64], start=(kc == 0), stop=(kc == 7)),
                     reads=[S.xmb, S.Wb], writes=[bk], skip_same=True)
            ti = (t0 + s * 128) // 128
            p.op("act", lambda e, bk=bk, ti=ti: e.activation(out=V1[:, ti, 0:64], in_=bk[:, 0:64], func=AF.Copy), reads=[bk], writes=[V1])
    t1, t2, t3 = S.t512
    rc, rs = S.io["ropeC"], S.io["ropeS"]
    for arr_, nname in ((q, "qn"), (k, "kn")):
        for bi, (t0, n) in enumerate(BLOCKS):
            bk = S.banks[bi % 2]
            p.op("act", lambda e, t0=t0, n=n, arr_=arr_: e.activation(out=t1[:, :n], in_=arr_[:, t0:t0 + n], func=AF.Square), reads=[arr_], writes=[t1])
            p.op("pe", lambda e, bk=bk, n=n: e.matmul(bk[:, :n], S.bm[:], t1[:, :n], start=True, stop=True), reads=[S.bm, t1], writes=[bk], skip_same=True)
            p.op("act", lambda e, bk=bk, n=n: e.activation(out=t2[:, :n], in_=bk[:, :n], func=AF.Sqrt, scale=1.0 / 64, bias=1e-6), reads=[bk], writes=[t2])
            p.op("dve", lambda e, n=n: e.reciprocal(out=t2[:, :n], in_=t2[:, :n]), reads=[t2], writes=[t2])
            p.op("dve", lambda e, t0=t0, n=n, arr_=arr_, nname=nname: e.scalar_tensor_tensor(out=arr_[:, t0:t0 + n], in0=arr_[:, t0:t0 + n], scalar=prm(S, nname), in1=t2[:, :n], op0=ALU.mult, op1=ALU.mult),
                 reads=[arr_, S.prm, t2], writes=[arr_])
            if t0 >= 256:
                x0 = t0 - 256
                p.dma("sp", lambda e, x0=x0, n=n: e.dma_start(out=t1[:, :n], in_=rc[:, x0:x0 + n]), reads=[rc], writes=[t1])
                p.dma("sp", lambda e, x0=x0, n=n: e.dma_start(out=t3[:, :n], in_=rs[:, x0:x0 + n]), reads=[rs], writes=[t3])
                bk2 = S.banks[2 + bi % 2]
                p.op("pe", lambda e, bk2=bk2, t0=t0, n=n, arr_=arr_: e.matmul(bk2[:, :n], perm[:], arr_[:, t0:t0 + n], start=True, stop=True), reads=[perm, arr_], writes=[bk2], skip_same=True)
                p.op("dve", lambda e, bk2=bk2, n=n: e.tensor_tensor(out=t3[:, :n], in0=bk2[:, :n], in1=t3[:, :n], op=ALU.mult), reads=[bk2, t3], writes=[t3])
                p.op("dve", lambda e, t0=t0, n=n, arr_=arr_: e.tensor_tensor(out=arr_[:, t0:t0 + n], in0=arr_[:, t0:t0 + n], in1=t1[:, :n], op=ALU.mult), reads=[arr_, t1], writes=[arr_])
                p.op("dve", lambda e, t0=t0, n=n, arr_=arr_: e.tensor_tensor(out=arr_[:, t0:t0 + n], in0=arr_[:, t0:t0 + n], in1=t3[:, :n], op=ALU.add), reads=[arr_, t3], writes=[arr_])
    ytm = p.alloc_sb("ytm", [128, 4, 128], F32)
    rec = p.alloc_sb("rec", [128, 4], F32)
    yv = S.io["yT"].t
    YB = S.ybase
    pp = 0
    for (t0, n) in BLOCKS:
        kchunks = [0, 1] if t0 < 256 else list(range(34))
        nq = n // 128
        for h in range(2):
            hs = slice(64 * h, 64 * h + 64)
            for ki, kc in enumerate(kchunks):
                sb = S.banks[pp % 2]
                pt = PT[pp % 2]
                pp += 1
                p.op("pe", lambda e, sb=sb, kc=kc, t0=t0, n=n, hs=hs: e.matmul(sb[:, :n], k[hs, kc * 128:(kc + 1) * 128], q[hs, t0:t0 + n], start=True, stop=True), reads=[k, q], writes=[sb], skip_same=True)
                p.op("act", lambda e, sb=sb, pt=pt, n=n: e.activation(out=pt[:, :n], in_=sb[:, :n], func=AF.Exp, scale=0.125), reads=[sb], writes=[pt])
                for qs in range(nq):
                    ob = S.banks[2 + qs]
                    p.op("pe", lambda e, ob=ob, pt=pt, qs=qs, kc=kc, ki=ki: e.matmul(ob[:, 0:65], pt[:, qs * 128:(qs + 1) * 128], V1[:, kc, :], start=(ki == 0), stop=(ki == len(kchunks) - 1)),
                         reads=[pt, V1], writes=[ob], skip_same=True)
            for qs in range(nq):
                ob = S.banks[2 + qs]
                p.op("dve", lambda e, ob=ob, qs=qs: e.reciprocal(out=rec[:, qs:qs + 1], in_=ob[:, 64:65]), reads=[ob], writes=[rec])
                p.op("dve", lambda e, ob=ob, qs=qs, h=h: e.tensor_scalar(out=ytm[:, qs, 64 * h:64 * h + 64], in0=ob[:, 0:64], scalar1=rec[:, qs:qs + 1], scalar2=None, op0=ALU.mult), reads=[ob, rec], writes=[ytm])
        tb = S.banks[6]
        for qs in range(nq):
            p.op("pe", lambda e, qs=qs: e.transpose(tb[:, qs * 128:(qs + 1) * 128], ytm[:, qs, :], S.ident[:]), reads=[ytm, S.ident], writes=[tb], skip_same=True)
        p.op("act", lambda e, n=n: e.activation(out=S.yb[:, :n], in_=tb[:, :n], func=AF.Copy), reads=[tb], writes=[S.yb])
        p.dma("sp", lambda e, t0=t0, n=n: e.dma_start(out=yv[YB + 3, :, t0:t0 + n], in_=S.yb[:, :n]), reads=[S.yb], writes=[S.io["yT"]])


def wout_stage(S, NM=4):
    p = S.p
    Wo = p.alloc_sb("Wo", [128, NM, 1024], BF16)
    yblk = p.alloc_sb("yblk", [128, NM, 512], BF16)
    for m in range(NM):
        p.dma("sp", lambda e, m=m: e.dma_start(out=S.stage[:, 0:1024], in_=S.io["wo"][m * 128:(m + 1) * 128, :]), reads=[S.io["wo"]], writes=[S.stage])
        p.op("pool", lambda e, m=m: e.tensor_copy(out=Wo[:, m, :], in_=S.stage[:, 0:1024]), reads=[S.stage], writes=[Wo])
    yv = S.io["yT"].t.rearrange("m p t -> p m t")
    pv = S.io["PT"].t.rearrange("(k p) t -> p k t", p=128)
    ob = 0
    for (t0, n) in BLOCKS:
        p.dma("sp", lambda e, t0=t0, n=n: e.dma_start(out=yblk[:, :, :n], in_=yv[:, :, t0:t0 + n]), reads=[S.io["yT"]], writes=[yblk])
        for dc in range(8):
            bk = S.banks[ob % 2]
            tt = S.t512[ob % 2]
            ob += 1
            for m in range(NM):
                p.op("pe", lambda e, bk=bk, m=m, dc=dc, n=n: e.matmul(bk[:, :n], Wo[:, m, dc * 128:(dc + 1) * 128], yblk[:, m, :n], start=(m == 0), stop=(m == NM - 1)), reads=[Wo, yblk], writes=[bk], skip_same=True)
            p.op("act", lambda e, bk=bk, tt=tt, n=n: e.activation(out=tt[:, :n], in_=bk[:, :n], func=AF.Copy), reads=[bk], writes=[tt])
            p.dma("sp", lambda e, tt=tt, dc=dc, t0=t0, n=n: e.dma_start(out=pv[:, dc, t0:t0 + n], in_=tt[:, :n]), reads=[tt], writes=[S.io["PT"]])


def build_k2(mixers=("gla", "ssd", "rwkv", "att"), test=True):
    nc = bass.Bass("TRN2", target_bir_lowering=False)
    p = Prog(nc)
    io = {}
    io["xmT"] = p.dram("xmT", [1024, T], BF16, "ExternalInput")
    io["wsel"] = p.dram("wsel", [1024, NW], F32, "ExternalInput")
    io["prm"] = p.dram("prm", [128, NPRM], F32, "ExternalInput")
    io["gwdec"] = p.dram("gwdec", [16, 2, 64], F32, "ExternalInput")
    io["rww"] = p.dram("rww", [128, 4, 128], F32, "ExternalInput")
    io["perm"] = p.dram("perm", [128, 128], F32, "ExternalInput")
    io["ropeC"] = p.dram("ropeC", [128, 4096], F32, "ExternalInput")
    io["ropeS"] = p.dram("ropeS", [128, 4096], F32, "ExternalInput")
    io["wo"] = p.dram("wo", [512, 1024], F32, "ExternalInput")
    io["yT"] = p.dram("yT", [4, 128, T], BF16, "ExternalOutput" if test else "Internal")
    io["PT"] = p.dram("PT", [1024, T], F32, "ExternalOutput")
    io["ssq0"] = p.dram("ssq0", [128, T], F32)
    io["yg0"] = p.dram("yg0", [128, T], F32)
    S = k2_setup(p, io)
    S.ssd_both = False
    der_setup(S)
    if "gla" in mixers:
        gla(S)
    if "ssd" in mixers:
        ssd(S)
    if "rwkv" in mixers:
        rwkv(S)
    if "att" in mixers:
        attn(S)
    if "wout" in mixers:
        wout_stage(S)
    p.finish()
    return nc

import numpy as np


def wsel_for(inp, l, hp):
    W = inp['w_in'][l]
    cols = {}
    cols["gq"] = np.arange(hp * 64, hp * 64 + 64)
    cols["gk"] = 128 + cols["gq"]
    cols["gv"] = 256 + np.arange(hp * 128, hp * 128 + 128)
    cols["gg"] = 512 + np.arange(hp * 128, hp * 128 + 128)
    cols["glr"] = 768 + np.arange(16)
    sb = 784
    for s in range(2):
        g = (1 - hp) if s == 0 else hp
        cols["sz%d" % s] = sb + np.arange(g * 128, g * 128 + 128)
        cols["sx%d" % s] = sb + 256 + np.arange(g * 128, g * 128 + 128)
        cols["sB%d" % s] = np.tile(sb + 256 + 256 + np.arange(g * 64, g * 64 + 64), 2)
        cols["sC%d" % s] = np.tile(sb + 256 + 384 + np.arange(g * 64, g * 64 + 64), 2)
        cols["sdt%d" % s] = sb + 768 + np.repeat(np.arange(2 * g, 2 * g + 2), 64)
    rb = 784 + 772
    cols["rr"] = rb + np.arange(hp * 128, hp * 128 + 128)
    cols["rk"] = rb + 256 + np.arange(hp * 128, hp * 128 + 128)
    cols["rv"] = rb + 512 + np.arange(hp * 128, hp * 128 + 128)
    cols["rlw"] = rb + 768 + np.arange(64)
    cols["rla"] = rb + 832 + np.arange(64)
    cols["rlg"] = rb + 896 + np.arange(128)
    ab = rb + 1024
    cols["aq"] = ab + np.arange(hp * 128, hp * 128 + 128)
    cols["ak"] = np.tile(ab + 256 + np.arange(hp * 64, hp * 64 + 64), 2)
    cols["av"] = ab + 384 + np.arange(hp * 64, hp * 64 + 64)
    out = np.zeros((1024, NW), np.float32)
    for nm, (c0, n) in WCOLS.items():
        out[:, c0:c0 + n] = W[:, cols[nm]]
    return out


def prm_for(inp, l, hp):
    P = np.zeros((128, NPRM), np.float32)

    def put(nm, v):
        v = np.asarray(v).reshape(-1)
        P[:len(v), PRM[nm]] = v
    put("gb0", inp['gla_b_dec'][l, 0, hp * 64:hp * 64 + 64])
    put("gb1", inp['gla_b_dec'][l, 1, hp * 64:hp * 64 + 64])
    put("gnorm", np.tile(inp['gla_norm'][l], 2))
    cw, cb = inp['ssm_conv_w'][l], inp['ssm_conv_b'][l]
    for s in range(2):
        g = (1 - hp) if s == 0 else hp
        chs = {"sx": np.arange(g * 128, g * 128 + 128), "sB": np.tile(256 + np.arange(g * 64, g * 64 + 64), 2), "sC": np.tile(384 + np.arange(g * 64, g * 64 + 64), 2)}
        for nm, ch in chs.items():
            for j in range(3):
                put("cw%d_%s%d" % (j, nm, s), cw[j, ch])
            put("cb_%s%d" % (nm, s), cb[ch])
        hd = np.repeat(np.arange(2 * g, 2 * g + 2), 64)
        for d in range(2):
            put("dtb%d_%d" % (d, s), inp['ssm_dt_bias'][l, d, hd])
            put("alog%d_%d" % (d, s), inp['ssm_a_log'][l, d, hd])
        put("D_%d" % s, inp['ssm_d'][l, hd])
    put("snorm", inp['ssm_norm'][l, hp * 128:hp * 128 + 128])
    put("snorm_o", inp['ssm_norm'][l, (1 - hp) * 128:(1 - hp) * 128 + 128])
    mu = inp['rwkv_shift_mu'][l]
    rc = {"rr": np.arange(hp * 128, hp * 128 + 128), "rk": 256 + np.arange(hp * 128, hp * 128 + 128), "rv": 512 + np.arange(hp * 128, hp * 128 + 128),
          "rlw": 768 + np.arange(64), "rla": 832 + np.arange(64), "rlg": 896 + np.arange(128)}
    for nm, ch in rc.items():
        put("mu0_" + nm, mu[0, ch])
        put("mu1_" + nm, mu[1, ch])
    own = slice(hp * 128, hp * 128 + 128)
    put("w0_0", inp['rwkv_w0'][l, 0, own])
    put("w0_1", inp['rwkv_w0'][l, 1, own])
    put("a0", inp['rwkv_a0'][l, own])
    put("k_k", inp['rwkv_k_k'][l, own])
    put("k_a", inp['rwkv_k_a'][l, own])
    put("r_k", inp['rwkv_r_k'][l, 2 * hp:2 * hp + 2])
    put("ln_g", inp['rwkv_ln_g'][l, own])
    put("ln_b", inp['rwkv_ln_b'][l, own])
    put("qn", np.tile(inp['att_q_norm'][l], 2))
    put("kn", np.tile(inp['att_k_norm'][l], 2))
    return P


def misc_for(inp, l, hp):
    m = {}
    m["gwdec"] = np.ascontiguousarray(inp['gla_w_dec'][l][:, :, hp * 64:hp * 64 + 64].transpose(1, 0, 2))
    own = slice(hp * 128, hp * 128 + 128)
    rww = np.zeros((128, 4, 128), np.float32)
    rww[0:64, 0] = inp['rwkv_w_dec'][l, 0][:, own]
    rww[0:64, 1] = inp['rwkv_w_dec'][l, 1][:, own]
    rww[0:64, 2] = inp['rwkv_w_a'][l][:, own]
    rww[:, 3] = inp['rwkv_w_g'][l][:, own]
    m["rww"] = rww
    Wo = inp['w_out'][l]
    m["wo"] = np.concatenate([Wo[0 + hp * 128:0 + hp * 128 + 128], Wo[256 + hp * 128:256 + hp * 128 + 128],
                              Wo[512 + hp * 128:512 + hp * 128 + 128], Wo[768 + hp * 128:768 + hp * 128 + 128]], 0)
    return m


def rope_consts():
    t = np.arange(4096)
    pos = np.stack([(t // 64).astype(np.float32), (t % 64).astype(np.float32)], 0)
    inv = (np.float32(10000.0) ** (-np.arange(0, 32, 2, dtype=np.float32) / np.float32(32))).astype(np.float32)
    C = np.zeros((128, 4096), np.float32)
    Sg = np.zeros((128, 4096), np.float32)
    perm = np.zeros((128, 128), np.float32)
    for h in range(2):
        for ax in range(2):
            for half in range(2):
                for i in range(16):
                    d = h * 64 + ax * 32 + half * 16 + i
                    ang = (pos[ax] * inv[i]).astype(np.float32)
                    C[d] = np.cos(ang)
                    Sg[d] = -np.sin(ang) if half == 0 else np.sin(ang)
                    partner = d + 16 if half == 0 else d - 16
                    perm[partner, d] = 1.0
    return {"ropeC": C, "ropeS": Sg, "perm": perm}


def wout_light(p, io, banks, NM):
    stage = p.alloc_sb("wstage", [128, 1024], F32)
    tt2 = [p.alloc_sb("wt%d" % i, [128, 512], F32) for i in range(2)]
    Wo = p.alloc_sb("Wo", [128, NM, 1024], BF16)
    yblk = p.alloc_sb("yblk", [128, NM, 512], BF16)
    for m in range(NM):
        p.dma("sp", lambda e, m=m: e.dma_start(out=stage[:, 0:1024], in_=io["wo"][m * 128:(m + 1) * 128, :]), reads=[io["wo"]], writes=[stage])
        p.op("pool", lambda e, m=m: e.tensor_copy(out=Wo[:, m, :], in_=stage[:, 0:1024]), reads=[stage], writes=[Wo])
    yv = io["yT"].t.rearrange("m p t -> p m t")
    pv = io["PT"].t.rearrange("(k p) t -> p k t", p=128)
    ob = 0
    for (t0, n) in BLOCKS:
        p.dma("sp", lambda e, t0=t0, n=n: e.dma_start(out=yblk[:, :, :n], in_=yv[:, :, t0:t0 + n]), reads=[io["yT"]], writes=[yblk])
        for dc in range(8):
            bk = banks[ob % 2]
            tt = tt2[ob % 2]
            ob += 1
            for m in range(NM):
                p.op("pe", lambda e, bk=bk, m=m, dc=dc, n=n: e.matmul(bk[:, :n], Wo[:, m, dc * 128:(dc + 1) * 128], yblk[:, m, :n], start=(m == 0), stop=(m == NM - 1)), reads=[Wo, yblk], writes=[bk], skip_same=True)
            p.op("act", lambda e, bk=bk, tt=tt, n=n: e.activation(out=tt[:, :n], in_=bk[:, :n], func=AF.Copy), reads=[bk], writes=[tt])
            p.dma("sp", lambda e, tt=tt, dc=dc, t0=t0, n=n: e.dma_start(out=pv[:, dc, t0:t0 + n], in_=tt[:, :n]), reads=[tt], writes=[io["PT"]])


def build_fused(NL=4, debug=False):
    nc = bass.Bass("TRN2", target_bir_lowering=False)
    p = Prog(nc)
    D = {}
    D["xT0"] = p.dram("xT0", [1024, T], F32, "ExternalInput")
    D["outT"] = p.dram("outT", [1024, T], F32, "ExternalOutput")
    D["cT"] = p.dram("cT", [128, 16], F32, "ExternalInput")
    D["bmT"] = p.dram("bmT", [NL, 128, 72], F32, "ExternalInput")
    D["ngT"] = p.dram("ngT", [NL, 128, 24], F32, "ExternalInput")
    D["w_mod"] = p.dram("w_mod", [NL, 1024, 9216], F32, "ExternalInput")
    D["ffn_in"] = p.dram("ffn_in", [NL, 2, 1024, 2 * DFF], F32, "ExternalInput")
    D["ffn_out"] = p.dram("ffn_out", [NL, 2, DFF, 1024], F32, "ExternalInput")
    D["wsel"] = p.dram("wsel", [NL, 2, 1024, NW], F32, "ExternalInput")
    D["prm"] = p.dram("prm", [NL, 2, 128, NPRM], F32, "ExternalInput")
    D["gwdec"] = p.dram("gwdec", [NL, 2, 16, 2, 64], F32, "ExternalInput")
    D["rww"] = p.dram("rww", [NL, 2, 128, 4, 128], F32, "ExternalInput")
    D["wo"] = p.dram("wo", [NL, 1024, 1024], F32, "ExternalInput")
    D["perm"] = p.dram("perm", [128, 128], F32, "ExternalInput")
    D["ropeC"] = p.dram("ropeC", [128, 4096], F32, "ExternalInput")
    D["ropeS"] = p.dram("ropeS", [128, 4096], F32, "ExternalInput")
    kd = "ExternalOutput" if debug else "Internal"
    xs = [p.dram("xsA", [1024, T], F32, kd), p.dram("xsB", [1024, T], F32)]
    xm = p.dram("xmS", [1024, T], BF16, kd)
    yT = p.dram("yTS", [8, 128, T], BF16, kd)
    PT = p.dram("PTS", [1024, T], F32, kd)
    ssq0 = p.dram("ssq0", [128, T], F32)
    yg0 = p.dram("yg0", [128, T], F32)
    banks = [p.alloc_ps("bank%d" % i, [128, 512], F32) for i in range(8)]
    ones = p.alloc_sb("gones", [128, 128], F32)
    p.op("pool", lambda e: e.memset(ones[:], 1.0), writes=[ones])
    G = {"banks": banks, "ones": ones}
    blocks = [(0, 256, 0)] + [(256 + 256 * i, 256, 1) for i in range(16)]

    def lv(nm, *idx):
        ap = D[nm].t
        for i in idx:
            ap = ap[i]
        return Buf(nm, ap)
    cur = D["xT0"]
    for l in range(NL):
        base = {"cT": D["cT"], "bmT": lv("bmT", l), "ngT": lv("ngT", l), "w_mod": lv("w_mod", l)}
        mk = p.mark()
        io = dict(base)
        io.update({"xT": cur, "yT": xs[0], "xmT": xm, "ffn_in": lv("ffn_in", l, 0), "ffn_out": lv("ffn_out", l, 0)})
        token_pass(p, io, False, True, 0, G, blocks)
        p.release(mk)
        mk = p.mark()
        io2 = {"xmT": xm, "yT": yT, "PT": PT, "ssq0": ssq0, "yg0": yg0, "perm": D["perm"], "ropeC": D["ropeC"], "ropeS": D["ropeS"],
               "wsel": lv("wsel", l, 0), "prm": lv("prm", l, 0), "gwdec": lv("gwdec", l, 0), "rww": lv("rww", l, 0)}
        S = k2_setup(p, io2, banks)
        S.ssd_both = True
        der_setup(S)
        for hp in range(2):
            if hp == 1:
                for nm in ("wsel", "prm", "gwdec", "rww"):
                    S.io[nm] = lv(nm, l, 1)
                load_prm(S)
            S.ybase = 4 * hp
            fns = [gla, rwkv, attn] if hp == 0 else [gla, ssd, rwkv, attn]
            for fn in fns:
                mk2 = p.mark()
                fn(S)
                p.release(mk2)
        p.release(mk)
        mk = p.mark()
        wout_light(p, {"wo": lv("wo", l), "yT": yT, "PT": PT}, banks, 8)
        p.release(mk)
        mk = p.mark()
        io = dict(base)
        dst = D["outT"] if l == NL - 1 else xs[1]
        io.update({"xT": xs[0], "yT": dst, "P0": PT, "ffn_in": lv("ffn_in", l, 1), "ffn_out": lv("ffn_out", l, 1)})
        token_pass(p, io, True, False, 1, G, blocks)
        p.release(mk)
        cur = xs[1]
    p.finish()
    return nc


def fused_inputs(inp, b, NL=4):
    x, ctx = inp['x'], inp['ctx']
    m = {}
    m["xT0"] = np.ascontiguousarray(np.concatenate([ctx[b], x[b]], 0).T)
    cT = np.stack([inp['c_ctx'], inp['c'][b]], -1)
    m["cT"] = np.ascontiguousarray(cT.reshape(8, 128, 2).transpose(1, 0, 2).reshape(128, 16))
    m["bmT"] = np.stack([np.ascontiguousarray(inp['b_mod'][l].reshape(72, 128).T) for l in range(NL)])
    m["ngT"] = np.stack([np.ascontiguousarray(inp['norm_g'][l].reshape(3, 8, 128).transpose(2, 0, 1).reshape(128, 24)) for l in range(NL)])
    m["w_mod"] = np.ascontiguousarray(inp['w_mod'][:NL])
    m["ffn_in"] = np.ascontiguousarray(inp['ffn_in'][:NL])
    m["ffn_out"] = np.ascontiguousarray(inp['ffn_out'][:NL])
    m["wsel"] = np.stack([np.stack([wsel_for(inp, l, hp) for hp in range(2)]) for l in range(NL)])
    m["prm"] = np.stack([np.stack([prm_for(inp, l, hp) for hp in range(2)]) for l in range(NL)])
    mi = [[misc_for(inp, l, hp) for hp in range(2)] for l in range(NL)]
    m["gwdec"] = np.stack([np.stack([mi[l][hp]["gwdec"] for hp in range(2)]) for l in range(NL)])
    m["rww"] = np.stack([np.stack([mi[l][hp]["rww"] for hp in range(2)]) for l in range(NL)])
    m["wo"] = np.stack([np.concatenate([mi[l][hp]["wo"] for hp in range(2)], 0) for l in range(NL)])
    m.update(rope_consts())
    return m


_FUSED = {}


def kernel(**inp):
    inp = {k: np.asarray(v) for k, v in inp.items()}
    if "nc" not in _FUSED:
        _FUSED["nc"] = build_fused(4)
    nc = _FUSED["nc"]
    per_b = [fused_inputs(inp, b) for b in range(4)]
    maps = [per_b[c // 2] for c in range(8)]
    res = run_bass_kernel_spmd(nc, maps, core_ids=list(range(8))).results
    out = np.zeros((4, 4096, 1024), np.float32)
    for b in range(4):
        out[b] = res[2 * b]["outT"][:, 256:].T
    return out
```

```python
import math
import numpy as np
import concourse.bass as bass
import concourse.mybir as mybir
from concourse.bass_utils import run_bass_kernel_spmd

F32 = mybir.dt.float32
BF16 = mybir.dt.bfloat16
I32 = mybir.dt.int32
AF = mybir.ActivationFunctionType
ALU = mybir.AluOpType
AX = mybir.AxisListType


class Buf:
    __slots__ = ("name", "w", "r", "t")

    def __init__(self, name, t=None):
        self.name = name
        self.w = None
        self.r = {}
        self.t = t

    def __getitem__(self, idx):
        return self.t[idx]


class Prog:
    ENGS = ("pe", "dve", "act", "pool", "sp")
    ROT = 30000

    def __init__(self, nc):
        self.nc = nc
        self.stack = []
        self.tstack = []
        self.streams = {e: [] for e in self.ENGS}
        self.sems = {}
        self.cur = {}
        self.waited = {e: {} for e in self.ENGS}
        self.nsem = 0
        for e in self.ENGS:
            self._new_eng_sem(e)
        self.dma_pool = {}
        self.dma_rr = {}
        for q, n in (("sp", 12), ("pool", 6), ("act", 6)):
            self.dma_pool[q] = [[self._new_sem("d%s%d" % (q, i)), 0] for i in range(n)]
            self.dma_rr[q] = 0
        self.ninst = 0

    def _new_sem(self, name):
        cm = self.nc.semaphore("s%d_%s" % (self.nsem, name))
        h = cm.__enter__()
        self.stack.append(cm)
        key = self.nsem
        self.nsem += 1
        self.sems[key] = h
        return key

    def _new_eng_sem(self, e):
        self.cur[e] = [self._new_sem(e), 0]

    def alloc_sb(self, name, shape, dt=F32):
        self.nalloc = getattr(self, "nalloc", 0) + 1
        cm = self.nc.sbuf_tensor("sb%d_%s" % (self.nalloc, name), list(shape), dt)
        t = cm.__enter__()
        self.tstack.append(cm)
        return Buf(name, t)

    def alloc_ps(self, name, shape, dt=F32):
        cm = self.nc.psum_tensor("ps_" + name, list(shape), dt)
        t = cm.__enter__()
        self.stack.append(cm)
        return Buf(name, t)

    def mark(self):
        return len(self.tstack)

    def barrier(self):
        toks = []
        for e in self.ENGS:
            k, v = self.cur[e]
            if v > 0:
                toks.append((k, v))
        for q, pool in self.dma_pool.items():
            for k, v in pool:
                if v > 0:
                    toks.append((k, v))
        for e in self.ENGS:
            waits = []
            for k, v in toks:
                if self.waited[e].get(k, 0) >= v:
                    continue
                self.waited[e][k] = v
                waits.append((k, v))
            self.streams[e].append((waits, None, None, 0))

    def release(self, mark):
        self.barrier()
        while len(self.tstack) > mark:
            cm = self.tstack.pop()
            cm.__exit__(None, None, None)

    def dram(self, name, shape, dt=F32, kind="Internal", addr_space=None):
        if addr_space is not None:
            t = self.nc.dram_tensor(name, list(shape), dt, kind=kind, addr_space=addr_space)
        else:
            t = self.nc.dram_tensor(name, list(shape), dt, kind=kind)
        return Buf(name, t.ap())

    def _collect(self, e, reads, writes, skip_same=False):
        need = {}

        def add(tok):
            if tok is None:
                return
            k, v = tok
            if need.get(k, 0) < v:
                need[k] = v
        for b in reads:
            add(b.w)
        for b in writes:
            add(b.w)
            for k, v in b.r.items():
                add((k, v))
        out = []
        wd = self.waited[e]
        for k, v in need.items():
            if skip_same and k == self.cur[e][0]:
                continue
            if wd.get(k, 0) >= v:
                continue
            wd[k] = v
            out.append((k, v))
        return out

    def _mark(self, tok, reads, writes):
        k, v = tok
        for b in reads:
            if b.r.get(k, 0) < v:
                b.r[k] = v
        for b in writes:
            b.w = tok
            b.r = {}

    def op(self, e, fn, reads=(), writes=(), skip_same=False):
        waits = self._collect(e, reads, writes, skip_same)
        cur = self.cur[e]
        if cur[1] >= self.ROT:
            self._new_eng_sem(e)
            cur = self.cur[e]
        cur[1] += 1
        tok = (cur[0], cur[1])
        self.streams[e].append((waits, fn, tok[0], 1))
        self._mark(tok, reads, writes)
        self.waited[e][tok[0]] = max(self.waited[e].get(tok[0], 0), 0)
        self.ninst += 1
        return tok

    def dma(self, q, fn, reads=(), writes=()):
        pool = self.dma_pool[q]
        i = self.dma_rr[q]
        self.dma_rr[q] = (i + 1) % len(pool)
        slot = pool[i]
        waits = self._collect(q, reads, writes)
        if slot[1] > 0 and self.waited[q].get(slot[0], 0) < slot[1]:
            self.waited[q][slot[0]] = slot[1]
            waits.append((slot[0], slot[1]))
        slot[1] += 16
        tok = (slot[0], slot[1])
        self.streams[q].append((waits, fn, tok[0], 16))
        self._mark(tok, reads, writes)
        self.ninst += 1
        return tok

    def finish(self):
        fin = []
        for q, pool in self.dma_pool.items():
            for k, v in pool:
                if v > 0:
                    fin.append((k, v))
        self.streams["sp"].append((fin, None, None, 0))
        nc = self.nc
        sems = self.sems
        streams = self.streams
        with nc.Block() as block:
            def emit(eng, lst):
                for waits, fn, sk, inc in lst:
                    for k, v in waits:
                        eng.wait_ge(sems[k], v)
                    if fn is not None:
                        fn(eng).then_inc(sems[sk], inc)

            @block.tensor
            def _(eng):
                emit(eng, streams["pe"])

            @block.vector
            def _(eng):
                emit(eng, streams["dve"])

            @block.scalar
            def _(eng):
                emit(eng, streams["act"])

            @block.gpsimd
            def _(eng):
                emit(eng, streams["pool"])

            @block.sync
            def _(eng):
                emit(eng, streams["sp"])
        for cm in reversed(self.tstack):
            cm.__exit__(None, None, None)
        for cm in reversed(self.stack):
            cm.__exit__(None, None, None)
        self.stack = []
        self.tstack = []


def view(buf, ap):
    return Buf(buf.name + "_v", ap)


TC = 2176
DFF = 2816
NJ = 22


def token_blocks():
    blks = [(0, 128, 0)]
    for i in range(8):
        blks.append((128 + 256 * i, 256, 1))
    return blks


def consts(p):
    c = {}
    c["ones"] = p.alloc_sb("ones", [128, 128], F32)
    p.op("pool", lambda e: e.memset(c["ones"][:], 1.0), writes=[c["ones"]])
    return c


def compute_mod(p, K, io, mlist, stage, bankM):
    scT = p.alloc_sb("scT", [128, 8, 2], F32)
    bmT = p.alloc_sb("bmT", [128, 72], F32)
    modT = p.alloc_sb("modT", [128, 9, 8, 2], F32)
    p.dma("sp", lambda e: e.dma_start(out=scT[:].rearrange("p k s -> p (k s)"), in_=io["cT"][:]), reads=[io["cT"]], writes=[scT])
    p.dma("sp", lambda e: e.dma_start(out=bmT[:], in_=io["bmT"][:]), reads=[io["bmT"]], writes=[bmT])
    p.op("act", lambda e: e.activation(out=scT[:], in_=scT[:], func=AF.Silu), reads=[scT], writes=[scT])
    si = 0
    for m in mlist:
        for kc in range(8):
            st = stage[si % 2]
            si += 1
            p.dma("sp", lambda e, st=st, kc=kc, m=m: e.dma_start(out=st[:, 0:1024], in_=io["w_mod"][kc * 128:(kc + 1) * 128, m * 1024:(m + 1) * 1024]),
                  reads=[io["w_mod"]], writes=[st])
            for j in range(8):
                p.op("pe", lambda e, st=st, kc=kc, j=j: e.matmul(bankM[:, j * 2:j * 2 + 2], st[:, j * 128:(j + 1) * 128], scT[:, kc, :],
                                                                start=(kc == 0 and j == 0), stop=(kc == 7 and j == 7), skip_group_check=True),
                     reads=[st, scT], writes=[bankM], skip_same=True)
        p.op("dve", lambda e, m=m: e.tensor_tensor(out=modT[:, m], in0=bankM[:, 0:16].rearrange("p (j s) -> p j s", s=2),
                                                  in1=bmT[:, m * 8:(m + 1) * 8].unsqueeze(2).broadcast_to([128, 8, 2]), op=ALU.add),
             reads=[bankM, bmT], writes=[modT])
    return modT


def load_cast(p, dst_ap_fn, src_ap_fn, ncols, stage, sidx, src_buf, dst_buf, q="sp"):
    st = stage[sidx % 2]
    p.dma(q, lambda e: e.dma_start(out=st[:, 0:ncols], in_=src_ap_fn()), reads=[src_buf], writes=[st])
    p.op("pool", lambda e: e.tensor_copy(out=dst_ap_fn(), in_=st[:, 0:ncols]), reads=[st], writes=[dst_buf])


def build_k1(pre_mix, post_mod, slot):
    nc = bass.Bass("TRN2", target_bir_lowering=False)
    p = Prog(nc)
    io = {}
    io["xT"] = p.dram("xT", [1024, TC], F32, "ExternalInput")
    io["cT"] = p.dram("cT", [128, 16], F32, "ExternalInput")
    io["bmT"] = p.dram("bmT", [128, 72], F32, "ExternalInput")
    io["ngT"] = p.dram("ngT", [128, 24], F32, "ExternalInput")
    io["w_mod"] = p.dram("w_mod", [1024, 9216], F32, "ExternalInput")
    io["ffn_in"] = p.dram("ffn_in", [1024, 2 * DFF], F32, "ExternalInput")
    io["ffn_out"] = p.dram("ffn_out", [DFF, 1024], F32, "ExternalInput")
    io["yT"] = p.dram("yT", [1024, TC], F32, "ExternalOutput")
    if pre_mix:
        io["P0"] = p.dram("P0", [1024, TC], F32, "ExternalInput")
        io["P1"] = p.dram("P1", [1024, TC], F32, "ExternalInput")
    if post_mod:
        io["xmT"] = p.dram("xmT", [1024, TC], BF16, "ExternalOutput")
    token_pass(p, io, pre_mix, post_mod, slot)
    p.finish()
    return nc


def token_pass(p, io, pre_mix, post_mod, slot, G=None, blocks=None):
    if G is None:
        K = consts(p)
        banks = [p.alloc_ps("bank%d" % i, [128, 512], F32) for i in range(8)]
    else:
        K = {"ones": G["ones"]}
        banks = G["banks"]
    if blocks is None:
        blocks = token_blocks()
    stage = [p.alloc_sb("stage%d" % i, [128, 1408], F32) for i in range(2)]
    mbase = 0 if slot == 0 else 6
    nidx = 0 if slot == 0 else 2
    mlist = [mbase, mbase + 1, mbase + 2]
    if pre_mix:
        mlist.append(5)
    if post_mod:
        mlist += [3, 4]
    modT = compute_mod(p, K, io, mlist, stage, banks[7])
    ngT = p.alloc_sb("ngT", [128, 3, 8], F32)
    p.dma("sp", lambda e: e.dma_start(out=ngT[:].rearrange("p n k -> p (n k)"), in_=io["ngT"][:]), reads=[io["ngT"]], writes=[ngT])
    geff = p.alloc_sb("geff", [128, 8, 2], F32)
    hg = p.alloc_sb("hg", [128, 8, 2], F32)

    def mk_geff(dst, n, msc):
        p.op("dve", lambda e: e.tensor_scalar(out=dst[:], in0=modT[:, msc], scalar1=1.0, scalar2=None, op0=ALU.add), reads=[modT], writes=[dst])
        p.op("dve", lambda e: e.tensor_tensor(out=dst[:], in0=dst[:], in1=ngT[:, n, :].unsqueeze(2).broadcast_to([128, 8, 2]), op=ALU.mult), reads=[dst, ngT], writes=[dst])
    mk_geff(geff, nidx, mbase + 1)
    p.op("dve", lambda e: e.tensor_scalar(out=hg[:], in0=modT[:, mbase + 2], scalar1=0.5, scalar2=None, op0=ALU.mult), reads=[modT], writes=[hg])
    if post_mod:
        geff2 = p.alloc_sb("geff2", [128, 8, 2], F32)
        mk_geff(geff2, 1, 4)
    Win = p.alloc_sb("Win", [128, 8, 2 * DFF], BF16)
    Wout = p.alloc_sb("Wout", [128, NJ, 1024], BF16)
    si = 0
    for kc in range(8):
        for q4 in range(4):
            load_cast(p, lambda kc=kc, q4=q4: Win[:, kc, q4 * 1408:(q4 + 1) * 1408],
                      lambda kc=kc, q4=q4: io["ffn_in"][kc * 128:(kc + 1) * 128, q4 * 1408:(q4 + 1) * 1408], 1408, stage, si, io["ffn_in"], Win)
            si += 1
    for j in range(NJ):
        load_cast(p, lambda j=j: Wout[:, j, :], lambda j=j: io["ffn_out"][j * 128:(j + 1) * 128, :], 1024, stage, si, io["ffn_out"], Wout)
        si += 1
    NB = 256
    xin = p.alloc_sb("xin", [128, 8, NB], F32)
    xm = p.alloc_sb("xm", [128, 8, NB], BF16)
    gT = p.alloc_sb("gT", [128, NJ, NB], BF16)
    sq = [p.alloc_sb("sq%d" % i, [128, NB], F32) for i in range(2)]
    tmp = [p.alloc_sb("tmp%d" % i, [128, NB], F32) for i in range(2)]
    sil = [p.alloc_sb("sil%d" % i, [128, NB], F32) for i in range(2)]
    rstd = p.alloc_sb("rstd", [128, NB], F32)
    if pre_mix:
        pm = [p.alloc_sb("pm%d" % i, [128, 8, NB], F32) for i in range(2 if "P1" in io else 1)]
    if post_mod:
        xm2 = p.alloc_sb("xm2", [128, 8, NB], BF16)
    xTv = io["xT"].t.rearrange("(k p) t -> p k t", p=128)
    yTv = io["yT"].t.rearrange("(k p) t -> p k t", p=128)

    def rms_mod(src, dst, ge, sh_m, n, s):
        bs = banks[6]
        for kc in range(8):
            t = sq[kc % 2]
            p.op("act", lambda e, t=t, kc=kc: e.activation(out=t[:, :n], in_=src[:, kc, :n], func=AF.Square), reads=[src], writes=[t])
            p.op("pe", lambda e, t=t, kc=kc: e.matmul(bs[:, :n], K["ones"][:], t[:, :n], start=(kc == 0), stop=(kc == 7)), reads=[K["ones"], t], writes=[bs], skip_same=True)
        p.op("act", lambda e: e.activation(out=rstd[:, :n], in_=bs[:, :n], func=AF.Sqrt, scale=1.0 / 1024, bias=1e-6), reads=[bs], writes=[rstd])
        p.op("dve", lambda e: e.reciprocal(out=rstd[:, :n], in_=rstd[:, :n]), reads=[rstd], writes=[rstd])
        for kc in range(8):
            t = tmp[kc % 2]
            p.op("dve", lambda e, t=t, kc=kc: e.tensor_tensor(out=t[:, :n], in0=src[:, kc, :n], in1=rstd[:, :n], op=ALU.mult), reads=[src, rstd], writes=[t])
            p.op("act", lambda e, t=t, kc=kc: e.activation(out=dst[:, kc, :n], in_=t[:, :n], func=AF.Identity, scale=ge[:, kc, s:s + 1], bias=modT[:, sh_m, kc, s:s + 1]),
                 reads=[t, ge, modT], writes=[dst])

    for (t0, n, s) in blocks:
        p.dma("sp", lambda e, t0=t0, n=n: e.dma_start(out=xin[:, :, :n], in_=xTv[:, :, t0:t0 + n]), reads=[io["xT"]], writes=[xin])
        if pre_mix:
            pnames = ("P0", "P1") if "P1" in io else ("P0",)
            for i, nm in enumerate(pnames):
                pv = io[nm].t.rearrange("(k p) t -> p k t", p=128)
                p.dma("sp", lambda e, i=i, pv=pv, t0=t0, n=n: e.dma_start(out=pm[i][:, :, :n], in_=pv[:, :, t0:t0 + n]), reads=[io[nm]], writes=[pm[i]])
            if len(pnames) == 2:
                p.op("pool", lambda e, n=n: e.tensor_tensor(out=pm[0][:, :, :n], in0=pm[0][:, :, :n], in1=pm[1][:, :, :n], op=ALU.add), reads=[pm[0], pm[1]], writes=[pm[0]])
            for kc in range(8):
                p.op("dve", lambda e, kc=kc, n=n, s=s: e.scalar_tensor_tensor(out=xin[:, kc, :n], in0=pm[0][:, kc, :n], scalar=modT[:, 5, kc, s:s + 1], in1=xin[:, kc, :n], op0=ALU.mult, op1=ALU.add),
                     reads=[pm[0], modT, xin], writes=[xin])
        rms_mod(xin, xm, geff, mbase, n, s)
        for j in range(NJ):
            ba, bb = banks[(j % 2) * 2], banks[(j % 2) * 2 + 1]
            for (bk, off) in ((ba, 0), (bb, DFF)):
                for kc in range(8):
                    p.op("pe", lambda e, bk=bk, off=off, kc=kc, j=j, n=n: e.matmul(bk[:, :n], Win[:, kc, off + j * 128:off + (j + 1) * 128], xm[:, kc, :n], start=(kc == 0), stop=(kc == 7)),
                         reads=[Win, xm], writes=[bk], skip_same=True)
            sl = sil[j % 2]
            p.op("act", lambda e, sl=sl, ba=ba, n=n: e.activation(out=sl[:, :n], in_=ba[:, :n], func=AF.Silu), reads=[ba], writes=[sl])
            p.op("dve", lambda e, sl=sl, bb=bb, j=j, n=n: e.tensor_tensor(out=gT[:, j, :n], in0=sl[:, :n], in1=bb[:, :n], op=ALU.mult), reads=[sl, bb], writes=[gT])
        for dc in range(8):
            bo = banks[4 + dc % 2]
            for j in range(NJ):
                p.op("pe", lambda e, bo=bo, dc=dc, j=j, n=n: e.matmul(bo[:, :n], Wout[:, j, dc * 128:(dc + 1) * 128], gT[:, j, :n], start=(j == 0), stop=(j == NJ - 1)),
                     reads=[Wout, gT], writes=[bo], skip_same=True)
            p.op("dve", lambda e, bo=bo, dc=dc, n=n, s=s: e.scalar_tensor_tensor(out=xin[:, dc, :n], in0=bo[:, :n], scalar=hg[:, dc, s:s + 1], in1=xin[:, dc, :n], op0=ALU.mult, op1=ALU.add),
                 reads=[bo, hg, xin], writes=[xin])
        p.dma("sp", lambda e, t0=t0, n=n: e.dma_start(out=yTv[:, :, t0:t0 + n], in_=xin[:, :, :n]), reads=[xin], writes=[io["yT"]])
        if post_mod:
            rms_mod(xin, xm2, geff2, 3, n, s)
            xmv = io["xmT"].t.rearrange("(k p) t -> p k t", p=128)
            p.dma("sp", lambda e, t0=t0, n=n, xmv=xmv: e.dma_start(out=xmv[:, :, t0:t0 + n], in_=xm2[:, :, :n]), reads=[xm2], writes=[io["xmT"]])

import math

T = 4352
TP = 4356
BLOCKS = [(0, 256)] + [(256 + 512 * i, 512) for i in range(8)]


def pcol(t0):
    return t0 + 1 if t0 < 256 else t0 + 3

WCOLS = {}
_c = 0
for nm, n in [("gq", 64), ("gk", 64), ("gv", 128), ("gg", 128), ("glr", 16),
              ("sz0", 128), ("sx0", 128), ("sB0", 128), ("sC0", 128), ("sdt0", 128),
              ("sz1", 128), ("sx1", 128), ("sB1", 128), ("sC1", 128), ("sdt1", 128),
              ("rr", 128), ("rk", 128), ("rv", 128), ("rlw", 64), ("rla", 64), ("rlg", 128),
              ("aq", 128), ("ak", 128), ("av", 64)]:
    WCOLS[nm] = (_c, n)
    _c += n
NW = _c

PRM = {}
_c = 0
for nm in ["gb0", "gb1", "gnorm",
           "cw0_sx0", "cw1_sx0", "cw2_sx0", "cb_sx0", "cw0_sB0", "cw1_sB0", "cw2_sB0", "cb_sB0", "cw0_sC0", "cw1_sC0", "cw2_sC0", "cb_sC0",
           "cw0_sx1", "cw1_sx1", "cw2_sx1", "cb_sx1", "cw0_sB1", "cw1_sB1", "cw2_sB1", "cb_sB1", "cw0_sC1", "cw1_sC1", "cw2_sC1", "cb_sC1",
           "dtb0_0", "dtb1_0", "alog0_0", "alog1_0", "D_0", "dtb0_1", "dtb1_1", "alog0_1", "alog1_1", "D_1", "snorm", "snorm_o",
           "mu0_rr", "mu1_rr", "mu0_rk", "mu1_rk", "mu0_rv", "mu1_rv", "mu0_rlw", "mu1_rlw", "mu0_rla", "mu1_rla", "mu0_rlg", "mu1_rlg",
           "w0_0", "w0_1", "a0", "k_k", "k_a", "r_k", "ln_g", "ln_b", "qn", "kn"]:
    PRM[nm] = _c
    _c += 1
NPRM = _c


class K2:
    pass


def load_prm(S):
    p = S.p
    src = S.io["prm"]
    p.dma("sp", lambda e: e.dma_start(out=S.prm[:], in_=src[:]), reads=[src], writes=[S.prm])


def k2_setup(p, io, banks=None):
    S = K2()
    S.p = p
    S.io = io
    S.ybase = 0
    S.banks = banks if banks is not None else [p.alloc_ps("bank%d" % i, [128, 512], F32) for i in range(8)]
    S.prm = p.alloc_sb("prm", [128, NPRM], F32)
    load_prm(S)
    S.ones = p.alloc_sb("ones", [128, 128], F32)
    p.op("pool", lambda e: e.memset(S.ones[:], 1.0), writes=[S.ones])
    S.bm = p.alloc_sb("bm", [128, 128], F32)
    p.op("pool", lambda e: e.memset(S.bm[:], 0.0), writes=[S.bm])
    p.op("pool", lambda e: e.memset(S.bm[0:64, 0:64], 1.0), reads=[S.bm], writes=[S.bm])
    p.op("pool", lambda e: e.memset(S.bm[64:128, 64:128], 1.0), reads=[S.bm], writes=[S.bm])
    S.bmg = p.alloc_sb("bmg", [128, 64], F32)
    p.op("pool", lambda e: e.memset(S.bmg[:], 0.0), writes=[S.bmg])
    p.op("pool", lambda e: e.memset(S.bmg[0:64, 0:32], 1.0), reads=[S.bmg], writes=[S.bmg])
    p.op("pool", lambda e: e.memset(S.bmg[64:128, 32:64], 1.0), reads=[S.bmg], writes=[S.bmg])
    S.I2 = p.alloc_sb("I2", [128, 64], F32)
    p.op("pool", lambda e: e.memset(S.I2[:], 1.0), writes=[S.I2])
    for hb in range(2):
        p.op("pool", lambda e, hb=hb: e.affine_select(out=S.I2[hb * 64:(hb + 1) * 64, :], in_=S.I2[hb * 64:(hb + 1) * 64, :], pattern=[[-1, 64]],
                                                      compare_op=ALU.is_equal, fill=0.0, base=0, channel_multiplier=1), reads=[S.I2], writes=[S.I2])
    S.ident = p.alloc_sb("ident", [128, 128], F32)
    p.op("pool", lambda e: e.memset(S.ident[:], 1.0), writes=[S.ident])
    p.op("pool", lambda e: e.affine_select(out=S.ident[:], in_=S.ident[:], pattern=[[-1, 128]], compare_op=ALU.is_equal, fill=0.0, base=0, channel_multiplier=1),
         reads=[S.ident], writes=[S.ident])
    S.stage = p.alloc_sb("stage", [128, 1024], F32)
    S.Wb = p.alloc_sb("Wb", [128, 8, 640], BF16)
    S.xmb = p.alloc_sb("xmb", [128, 8, 512], BF16)
    S.arr = [p.alloc_sb("arr%d" % i, [128, TP], F32) for i in range(8)]
    S.Vd = [p.alloc_sb("Vd%d" % i, [128, 4, 64], F32) for i in range(4)]
    S.Y0 = [p.alloc_sb("Y0%d" % i, [128, 4, 64], F32) for i in range(4)]
    S.Abd = [p.alloc_sb("Abd%d" % i, [128, 4, 128], F32) for i in range(4)]
    S.ST = [p.alloc_sb("ST%d" % i, [128, 64], F32) for i in range(8)]
    S.Z = [p.alloc_sb("Z%d" % i, [128, 64], F32) for i in range(2)]
    S.t512 = [p.alloc_sb("t512_%d" % i, [128, 512], F32) for i in range(3)]
    S.yb = p.alloc_sb("yb", [128, 512], BF16)
    return S


def prm(S, nm, np_=128):
    c = PRM[nm]
    return S.prm[0:np_, c:c + 1]


def load_w(S, names):
    p = S.p
    wsrc = S.io["wsel"]
    offs = {}
    o = 0
    for nm in names:
        c0, n = WCOLS[nm]
        offs[nm] = (o, n)
        for kc in range(8):
            p.dma("sp", lambda e, kc=kc, c0=c0, n=n: e.dma_start(out=S.stage[:, 0:n], in_=wsrc[kc * 128:(kc + 1) * 128, c0:c0 + n]), reads=[wsrc], writes=[S.stage])
            p.op("pool", lambda e, kc=kc, o=o, n=n: e.tensor_copy(out=S.Wb[:, kc, o:o + n], in_=S.stage[:, 0:n]), reads=[S.stage], writes=[S.Wb])
        o += n
    assert o <= 640
    return offs


def proj_fm(S, offs, dsts):
    p = S.p
    xv = S.io["xmT"].t.rearrange("(k p) t -> p k t", p=128)
    bi = 0
    for (t0, n) in BLOCKS:
        p.dma("sp", lambda e, t0=t0, n=n: e.dma_start(out=S.xmb[:, :, :n], in_=xv[:, :, t0:t0 + n]), reads=[S.io["xmT"]], writes=[S.xmb])
        for (nm, dst, padded) in dsts:
            o, m = offs[nm]
            bk = S.banks[bi % 2]
            bi += 1
            for kc in range(8):
                p.op("pe", lambda e, bk=bk, kc=kc, o=o, m=m, n=n: e.matmul(bk[0:m, :n], S.Wb[:, kc, o:o + m], S.xmb[:, kc, :n], start=(kc == 0), stop=(kc == 7)),
                     reads=[S.Wb, S.xmb], writes=[bk], skip_same=True)
            c = pcol(t0) if padded else t0
            p.op("act", lambda e, bk=bk, m=m, n=n, c=c, dst=dst: e.activation(out=dst[0:m, c:c + n], in_=bk[0:m, :n], func=AF.Copy), reads=[bk], writes=[dst])


def zero_pads(S, buf):
    p = S.p
    for c in (0, 257, 258, 4355):
        p.op("pool", lambda e, c=c: e.memset(buf[:, c:c + 1], 0.0), reads=[buf], writes=[buf])


def conv3(S, src, dst, np_, w0, w1, w2, bias=None):
    p = S.p
    for (i, o, n) in ((0, 0, 256), (258, 256, 4096)):
        if bias is not None:
            p.op("dve", lambda e, i=i, o=o, n=n: e.tensor_scalar(out=dst[0:np_, o:o + n], in0=src[0:np_, i + 1:i + 1 + n], scalar1=w1, scalar2=bias, op0=ALU.mult, op1=ALU.add),
                 reads=[src, S.prm], writes=[dst])
        else:
            p.op("dve", lambda e, i=i, o=o, n=n: e.tensor_scalar(out=dst[0:np_, o:o + n], in0=src[0:np_, i + 1:i + 1 + n], scalar1=w1, scalar2=None, op0=ALU.mult),
                 reads=[src, S.prm], writes=[dst])
        p.op("dve", lambda e, i=i, o=o, n=n: e.scalar_tensor_tensor(out=dst[0:np_, o:o + n], in0=src[0:np_, i:i + n], scalar=w0, in1=dst[0:np_, o:o + n], op0=ALU.mult, op1=ALU.add),
             reads=[src, dst, S.prm], writes=[dst])
        p.op("dve", lambda e, i=i, o=o, n=n: e.scalar_tensor_tensor(out=dst[0:np_, o:o + n], in0=src[0:np_, i + 2:i + 2 + n], scalar=w2, in1=dst[0:np_, o:o + n], op0=ALU.mult, op1=ALU.add),
             reads=[src, dst, S.prm], writes=[dst])


def scan_gen(S, ci, NP, w, k, v, r, heads, ab, bmV, dirn, yacc, ybank, vbank, sabank, yadd, vmul=None):
    p = S.p
    TB = 4
    ST = [S.ST[ci * 4 + i] for i in range(4)]
    Vd = [S.Vd[ci * 2], S.Vd[ci * 2 + 1]]
    Y0 = [S.Y0[ci * 2], S.Y0[ci * 2 + 1]]
    Z = S.Z[ci]
    Abd = [S.Abd[ci * 2], S.Abd[ci * 2 + 1]]
    p.op("pool", lambda e: e.memset(ST[0][:], 0.0), reads=[ST[0]], writes=[ST[0]])
    cur = 0
    if dirn == 0:
        blocks = BLOCKS
    else:
        blocks = [BLOCKS[0]] + BLOCKS[:0:-1]
    blist = []
    for (t0, n) in blocks:
        batches = list(range(t0, t0 + n, TB))
        if dirn == 1:
            batches = batches[::-1]
        for bi_, tb0 in enumerate(batches):
            blist.append((t0, n, tb0, bi_ == len(batches) - 1))

    def prep(nb):
        t0, n, tb0, last = blist[nb]
        vd, y0 = Vd[nb % 2], Y0[nb % 2]
        abd = Abd[nb % 2]
        p.op("pool", lambda e: e.tensor_tensor(out=vd[:], in0=S.I2[:].unsqueeze(1).broadcast_to([128, TB, 64]),
                                               in1=v[:, tb0:tb0 + TB].unsqueeze(2).broadcast_to([128, TB, 64]), op=ALU.mult), reads=[S.I2, v], writes=[vd])
        if vmul is not None:
            p.op("pool", lambda e: e.tensor_tensor(out=vd[:], in0=vd[:], in1=vmul[:, tb0:tb0 + TB].unsqueeze(2).broadcast_to([128, TB, 64]), op=ALU.mult), reads=[vd, vmul], writes=[vd])
        p.op("pe", lambda e: e.matmul(vbank[0:NP, 0:TB * 64], bmV[:, 0:NP], vd[:].rearrange("p a b -> p (a b)"), start=True, stop=True), reads=[bmV, vd], writes=[vbank], skip_same=True)
        p.op("dve", lambda e: e.tensor_tensor(out=y0[0:NP], in0=vbank[0:NP, 0:TB * 64].rearrange("p (a b) -> p a b", b=64),
                                              in1=k[0:NP, tb0:tb0 + TB].unsqueeze(2).broadcast_to([NP, TB, 64]), op=ALU.mult), reads=[vbank, k], writes=[y0])
        if ab is not None:
            p.op("pool", lambda e: e.tensor_tensor(out=abd[:], in0=S.bm[:].unsqueeze(1).broadcast_to([128, TB, 128]),
                                                   in1=ab[0][:, tb0:tb0 + TB].unsqueeze(2).broadcast_to([128, TB, 128]), op=ALU.mult), reads=[S.bm, ab[0]], writes=[abd])

    pend = []

    def flush_y():
        while pend:
            sb_, t_, col_ = pend.pop(0)
            for (kr, vr) in heads:
                p.op("pe", lambda e, sb_=sb_, kr=kr, vr=vr, col_=col_, t_=t_: e.matmul(ybank[vr[0]:vr[1], col_:col_ + 1], sb_[kr[0]:kr[1], :], r[kr[0]:kr[1], t_:t_ + 1], start=True, stop=True),
                     reads=[sb_, r], writes=[ybank], skip_same=True)

    prep(0)
    for nb in range(len(blist)):
        t0, n, tb0, last = blist[nb]
        y0 = Y0[nb % 2]
        abd = Abd[nb % 2]
        steps = list(range(tb0, tb0 + TB))
        if dirn == 1:
            steps = steps[::-1]
        for si, t in enumerate(steps):
            if si == 1 and nb + 1 < len(blist):
                prep(nb + 1)
            tt = t - tb0
            s_in, s_out = ST[cur], ST[(cur + 1) % 4]
            cur = (cur + 1) % 4
            if ab is not None:
                p.op("dve", lambda e, s_in=s_in, t=t, tt=tt, y0=y0: e.scalar_tensor_tensor(out=Z[0:NP], in0=s_in[0:NP], scalar=w[0:NP, t:t + 1], in1=y0[0:NP, tt, :], op0=ALU.mult, op1=ALU.add),
                     reads=[s_in, w, y0], writes=[Z])
                p.op("pe", lambda e, s_in=s_in, tt=tt, abd=abd: e.matmul(sabank[0:NP, 0:64], abd[:, tt, :], s_in[:], start=True, stop=True), reads=[abd, s_in], writes=[sabank], skip_same=True)
                flush_y()
                p.op("dve", lambda e, s_out=s_out, t=t: e.scalar_tensor_tensor(out=s_out[0:NP], in0=sabank[0:NP, 0:64], scalar=ab[1][0:NP, t:t + 1], in1=Z[0:NP], op0=ALU.mult, op1=ALU.add),
                     reads=[sabank, ab[1], Z], writes=[s_out])
            else:
                p.op("dve", lambda e, s_in=s_in, s_out=s_out, t=t, tt=tt, y0=y0: e.scalar_tensor_tensor(out=s_out[0:NP], in0=s_in[0:NP], scalar=w[0:NP, t:t + 1], in1=y0[0:NP, tt, :], op0=ALU.mult, op1=ALU.add),
                     reads=[s_in, w, y0], writes=[s_out])
                flush_y()
            pend.append((s_out, t, t - t0))
            yield
        if last:
            flush_y()
            if yadd:
                p.op("dve", lambda e, t0=t0, n=n: e.tensor_tensor(out=yacc[:, t0:t0 + n], in0=ybank[:, 0:n], in1=yacc[:, t0:t0 + n], op=ALU.add), reads=[ybank, yacc], writes=[yacc])
            else:
                p.op("act", lambda e, t0=t0, n=n: e.activation(out=yacc[:, t0:t0 + n], in_=ybank[:, 0:n], func=AF.Copy), reads=[ybank], writes=[yacc])


def run_gens(gens):
    alive = list(gens)
    while alive:
        for g in list(alive):
            try:
                next(g)
            except StopIteration:
                alive.remove(g)


def gla(S):
    p = S.p
    A = S.arr
    q, k, lr, wd, v, y = A[0], A[1], A[2], A[3], A[4], A[5]
    wdb = A[6]
    offs = load_w(S, ["gq", "gk", "gv", "glr"])
    proj_fm(S, offs, [("gq", q, False), ("gk", k, False), ("gv", v, False), ("glr", lr, False)])
    p.op("dve", lambda e: e.tensor_scalar(out=q[0:64, 0:T], in0=q[0:64, 0:T], scalar1=32 ** -0.5, scalar2=None, op0=ALU.mult), reads=[q], writes=[q])
    gw = p.alloc_sb("gwdec", [16, 2, 64], F32)
    gsrc = S.io["gwdec"]
    p.dma("sp", lambda e: e.dma_start(out=gw[:], in_=gsrc[:]), reads=[gsrc], writes=[gw])
    for d, dst in ((0, wd), (1, wdb)):
        for bi, (t0, n) in enumerate(BLOCKS):
            bk = S.banks[bi % 2]
            p.op("pe", lambda e, bk=bk, d=d, t0=t0, n=n: e.matmul(bk[0:64, :n], gw[:, d, :], lr[0:16, t0:t0 + n], start=True, stop=True), reads=[gw, lr], writes=[bk], skip_same=True)
            p.op("act", lambda e, bk=bk, d=d, t0=t0, n=n, dst=dst: e.activation(out=dst[0:64, t0:t0 + n], in_=bk[0:64, :n], func=AF.Exp, scale=-1.0, bias=None) if False else
                 e.activation(out=dst[0:64, t0:t0 + n], in_=bk[0:64, :n], func=AF.Identity, bias=prm(S, "gb%d" % d, 64), scale=1.0), reads=[bk, S.prm], writes=[dst])
        p.op("act", lambda e, dst=dst: e.activation(out=dst[0:64, 0:T], in_=dst[0:64, 0:T], func=AF.Exp, scale=-1.0), reads=[dst], writes=[dst])
        p.op("act", lambda e, dst=dst: e.activation(out=dst[0:64, 0:T], in_=dst[0:64, 0:T], func=AF.Ln, bias=1.0), reads=[dst], writes=[dst])
        p.op("act", lambda e, dst=dst: e.activation(out=dst[0:64, 0:T], in_=dst[0:64, 0:T], func=AF.Exp, scale=-1.0 / 16), reads=[dst], writes=[dst])
    heads = [((0, 32), (0, 64)), ((32, 64), (64, 128))]
    p.op("pool", lambda e: e.memset(y[:, 0:T], 0.0), reads=[y], writes=[y])
    yb0, yb1, vb0, vb1 = S.banks[2], S.banks[3], S.banks[4], S.banks[5]
    g0 = scan_gen(S, 0, 64, wd, k, v, q, heads, None, S.bmg, 0, y, yb0, vb0, None, True)
    g1 = scan_gen(S, 1, 64, wdb, k, v, q, heads, None, S.bmg, 1, y, yb1, vb1, None, True)
    run_gens([g0, g1])
    gg = A[0]
    offs = load_w(S, ["gg"])
    proj_fm(S, offs, [("gg", gg, False)])
    finish_rms_gate(S, y, gg, "gnorm", 0, 1.0 / 64, S.bm)


def finish_rms_gate(S, y, g, normname, m, inv_n, bmat, extra_ssq=None):
    p = S.p
    t1, t2, t3 = S.t512
    yv = S.io["yT"].t
    YB = S.ybase
    for bi, (t0, n) in enumerate(BLOCKS):
        bk = S.banks[bi % 2]
        p.op("act", lambda e, t0=t0, n=n: e.activation(out=t1[:, :n], in_=y[:, t0:t0 + n], func=AF.Square), reads=[y], writes=[t1])
        p.op("pe", lambda e, bk=bk, n=n: e.matmul(bk[:, :n], bmat[:], t1[:, :n], start=True, stop=True), reads=[bmat, t1], writes=[bk], skip_same=True)
        p.op("act", lambda e, bk=bk, n=n: e.activation(out=t2[:, :n], in_=bk[:, :n], func=AF.Sqrt, scale=inv_n, bias=1e-6), reads=[bk], writes=[t2])
        p.op("dve", lambda e, n=n: e.reciprocal(out=t2[:, :n], in_=t2[:, :n]), reads=[t2], writes=[t2])
        p.op("act", lambda e, t0=t0, n=n: e.activation(out=t3[:, :n], in_=g[:, t0:t0 + n], func=AF.Silu), reads=[g], writes=[t3])
        p.op("dve", lambda e, t0=t0, n=n: e.scalar_tensor_tensor(out=t2[:, :n], in0=t2[:, :n], scalar=prm(S, normname), in1=y[:, t0:t0 + n], op0=ALU.mult, op1=ALU.mult), reads=[t2, S.prm, y], writes=[t2])
        p.op("dve", lambda e, n=n: e.tensor_tensor(out=S.yb[:, :n], in0=t2[:, :n], in1=t3[:, :n], op=ALU.mult), reads=[t2, t3], writes=[S.yb])
        p.dma("sp", lambda e, t0=t0, n=n: e.dma_start(out=yv[YB + m, :, t0:t0 + n], in_=S.yb[:, :n]), reads=[S.yb], writes=[S.io["yT"]])


def der_setup(S):
    p = S.p
    S.der = p.alloc_sb("der", [128, 24], F32)
    S.dn = {}

    def col(nm):
        if nm not in S.dn:
            S.dn[nm] = len(S.dn)
        c = S.dn[nm]
        return S.der[:, c:c + 1]
    S.dcol = col


def ssd_group(S, slot, own, ssq_o):
    p = S.p
    A = S.arr
    P, xs, Bk, Cr, dtr, wd, dtb, wdb = A[0], A[1], A[2], A[3], A[4], A[5], A[6], A[7]
    g = str(slot)
    offs = load_w(S, ["sx" + g, "sB" + g, "sC" + g, "sdt" + g])
    zero_pads(S, P)
    for nm, dst in (("sx" + g, xs), ("sB" + g, Bk), ("sC" + g, Cr)):
        proj_fm(S, offs, [(nm, P, True)])
        conv3(S, P, dst, 128, prm(S, "cw0_" + nm), prm(S, "cw1_" + nm), prm(S, "cw2_" + nm), prm(S, "cb_" + nm))
        p.op("act", lambda e, dst=dst: e.activation(out=dst[:, 0:T], in_=dst[:, 0:T], func=AF.Silu), reads=[dst], writes=[dst])
    proj_fm(S, offs, [("sdt" + g, dtr, False)])
    y = P
    p.op("pool", lambda e: e.memset(y[:, 0:T], 0.0), reads=[y], writes=[y])
    heads = [((0, 64), (0, 64)), ((64, 128), (64, 128))]
    for d, dts, wds_ in ((1, dtb, wdb), (0, dtr, wd)):
        na = S.dcol("na%d_%s" % (d, g))
        p.op("act", lambda e, d=d, na=na: e.activation(out=na, in_=prm(S, "alog%d_%s" % (d, g)), func=AF.Exp), reads=[S.prm], writes=[S.der])
        p.op("dve", lambda e, na=na: e.tensor_scalar(out=na, in0=na, scalar1=-1.0, scalar2=None, op0=ALU.mult), reads=[S.der], writes=[S.der])
        p.op("act", lambda e, d=d, dts=dts: e.activation(out=dts[:, 0:T], in_=dtr[:, 0:T], func=AF.Exp, bias=prm(S, "dtb%d_%s" % (d, g)), scale=1.0), reads=[dtr, S.prm], writes=[dts])
        p.op("act", lambda e, dts=dts: e.activation(out=dts[:, 0:T], in_=dts[:, 0:T], func=AF.Ln, bias=1.0), reads=[dts], writes=[dts])
        p.op("act", lambda e, na=na, dts=dts, wds_=wds_: e.activation(out=wds_[:, 0:T], in_=dts[:, 0:T], func=AF.Exp, scale=na), reads=[dts, S.der], writes=[wds_])
    g0 = scan_gen(S, 0, 128, wd, Bk, xs, Cr, heads, None, S.bm, 0, y, S.banks[2], S.banks[4], None, True, vmul=dtr)
    g1 = scan_gen(S, 1, 128, wdb, Bk, xs, Cr, heads, None, S.bm, 1, y, S.banks[3], S.banks[6], None, True, vmul=dtb)
    run_gens([g0, g1])
    z = wd
    offs = load_w(S, ["sz" + g])
    proj_fm(S, offs, [("sz" + g, z, False)])
    p.op("dve", lambda e: e.scalar_tensor_tensor(out=y[:, 0:T], in0=xs[:, 0:T], scalar=prm(S, "D_" + g), in1=y[:, 0:T], op0=ALU.mult, op1=ALU.add), reads=[xs, S.prm, y], writes=[y])
    p.op("act", lambda e: e.activation(out=z[:, 0:T], in_=z[:, 0:T], func=AF.Silu), reads=[z], writes=[z])
    p.op("dve", lambda e: e.tensor_tensor(out=y[:, 0:T], in0=y[:, 0:T], in1=z[:, 0:T], op=ALU.mult), reads=[y, z], writes=[y])
    t1, t2, t3 = S.t512
    yv = S.io["yT"].t
    YB = S.ybase
    sq0, yg0 = S.io["ssq0"], S.io["yg0"]
    for bi, (t0, n) in enumerate(BLOCKS):
        bk = S.banks[bi % 2]
        p.op("act", lambda e, t0=t0, n=n: e.activation(out=t1[:, :n], in_=y[:, t0:t0 + n], func=AF.Square), reads=[y], writes=[t1])
        p.op("pe", lambda e, bk=bk, n=n: e.matmul(bk[:, :n], S.ones[:], t1[:, :n], start=True, stop=True), reads=[S.ones, t1], writes=[bk], skip_same=True)
        if not own:
            p.op("act", lambda e, bk=bk, n=n: e.activation(out=t2[:, :n], in_=bk[:, :n], func=AF.Copy), reads=[bk], writes=[t2])
            p.dma("sp", lambda e, t0=t0, n=n: e.dma_start(out=sq0[:, t0:t0 + n], in_=t2[:, :n]), reads=[t2], writes=[sq0])
            p.dma("sp", lambda e, t0=t0, n=n: e.dma_start(out=yg0[:, t0:t0 + n], in_=y[:, t0:t0 + n]), reads=[y], writes=[yg0])
        else:
            p.dma("sp", lambda e, t0=t0, n=n: e.dma_start(out=t3[:, :n], in_=sq0[:, t0:t0 + n]), reads=[sq0], writes=[t3])
            p.op("dve", lambda e, bk=bk, n=n: e.tensor_tensor(out=t2[:, :n], in0=bk[:, :n], in1=t3[:, :n], op=ALU.add), reads=[bk, t3], writes=[t2])
            p.op("act", lambda e, n=n: e.activation(out=t2[:, :n], in_=t2[:, :n], func=AF.Sqrt, scale=1.0 / 256, bias=1e-6), reads=[t2], writes=[t2])
            p.op("dve", lambda e, n=n: e.reciprocal(out=t2[:, :n], in_=t2[:, :n]), reads=[t2], writes=[t2])
            p.op("dve", lambda e, t0=t0, n=n: e.scalar_tensor_tensor(out=S.yb[:, :n], in0=t2[:, :n], scalar=prm(S, "snorm"), in1=y[:, t0:t0 + n], op0=ALU.mult, op1=ALU.mult), reads=[t2, S.prm, y], writes=[S.yb])
            p.dma("sp", lambda e, t0=t0, n=n: e.dma_start(out=yv[YB + 1, :, t0:t0 + n], in_=S.yb[:, :n]), reads=[S.yb], writes=[S.io["yT"]])
            if S.ssd_both:
                p.dma("sp", lambda e, t0=t0, n=n: e.dma_start(out=t3[:, :n], in_=yg0[:, t0:t0 + n]), reads=[yg0], writes=[t3])
                p.op("dve", lambda e, n=n: e.scalar_tensor_tensor(out=S.yb[:, :n], in0=t2[:, :n], scalar=prm(S, "snorm_o"), in1=t3[:, :n], op0=ALU.mult, op1=ALU.mult), reads=[t2, S.prm, t3], writes=[S.yb])
                p.dma("sp", lambda e, t0=t0, n=n: e.dma_start(out=yv[YB - 4 + 1, :, t0:t0 + n], in_=S.yb[:, :n]), reads=[S.yb], writes=[S.io["yT"]])


def ssd(S):
    ssd_group(S, 0, False, None)
    ssd_group(S, 1, True, None)


def rwkv(S):
    p = S.p
    A = S.arr
    P, r, k, v, A4, A5, A6, A7 = A
    rw = p.alloc_sb("rww", [128, 4, 128], F32)
    rsrc = S.io["rww"]
    p.dma("sp", lambda e: e.dma_start(out=rw[:], in_=rsrc[:]), reads=[rsrc], writes=[rw])

    def shift(nm, dst, np_):
        w1 = S.dcol("w1_" + nm)
        p.op("dve", lambda e: e.tensor_scalar(out=w1, in0=prm(S, "mu0_" + nm), scalar1=-1.0, scalar2=1.0, op0=ALU.mult, op1=ALU.add), reads=[S.prm], writes=[S.der])
        p.op("dve", lambda e: e.tensor_tensor(out=w1, in0=w1, in1=prm(S, "mu1_" + nm), op=ALU.subtract), reads=[S.prm, S.der], writes=[S.der])
        conv3(S, P, dst, np_, prm(S, "mu0_" + nm, np_), w1[0:np_], prm(S, "mu1_" + nm, np_))

    offs = load_w(S, ["rr", "rk", "rv", "rla", "rlw"])
    zero_pads(S, P)
    for nm, dst, np_ in (("rr", r, 128), ("rk", k, 128), ("rv", v, 128), ("rla", A4, 64)):
        proj_fm(S, offs, [(nm, P, True)])
        shift(nm, dst, np_)
    for bi, (t0, n) in enumerate(BLOCKS):
        bk = S.banks[bi % 2]
        p.op("pe", lambda e, bk=bk, t0=t0, n=n: e.matmul(bk[:, :n], rw[0:64, 2, :], A4[0:64, t0:t0 + n], start=True, stop=True), reads=[rw, A4], writes=[bk], skip_same=True)
        p.op("act", lambda e, bk=bk, t0=t0, n=n: e.activation(out=A5[:, t0:t0 + n], in_=bk[:, :n], func=AF.Sigmoid, bias=prm(S, "a0"), scale=1.0), reads=[bk, S.prm], writes=[A5])
    p.op("dve", lambda e: e.tensor_scalar(out=A6[:, 0:T], in0=k[:, 0:T], scalar1=prm(S, "k_k"), scalar2=None, op0=ALU.mult), reads=[k, S.prm], writes=[A6])
    t1, t2, t3 = S.t512
    for bi, (t0, n) in enumerate(BLOCKS):
        bk = S.banks[bi % 2]
        p.op("act", lambda e, t0=t0, n=n: e.activation(out=t1[:, :n], in_=A6[:, t0:t0 + n], func=AF.Square), reads=[A6], writes=[t1])
        p.op("pe", lambda e, bk=bk, n=n: e.matmul(bk[:, :n], S.bm[:], t1[:, :n], start=True, stop=True), reads=[S.bm, t1], writes=[bk], skip_same=True)
        p.op("act", lambda e, bk=bk, n=n: e.activation(out=t2[:, :n], in_=bk[:, :n], func=AF.Sqrt), reads=[bk], writes=[t2])
        p.op("dve", lambda e, n=n: e.tensor_scalar(out=t2[:, :n], in0=t2[:, :n], scalar1=1e-12, scalar2=None, op0=ALU.max), reads=[t2], writes=[t2])
        p.op("dve", lambda e, n=n: e.reciprocal(out=t2[:, :n], in_=t2[:, :n]), reads=[t2], writes=[t2])
        p.op("dve", lambda e, t0=t0, n=n: e.tensor_tensor(out=A6[:, t0:t0 + n], in0=A6[:, t0:t0 + n], in1=t2[:, :n], op=ALU.mult), reads=[A6, t2], writes=[A6])
    p.op("dve", lambda e: e.tensor_tensor(out=A4[:, 0:T], in0=A6[:, 0:T], in1=A5[:, 0:T], op=ALU.mult), reads=[A6, A5], writes=[A4])
    omka = S.dcol("omka")
    p.op("dve", lambda e: e.tensor_scalar(out=omka, in0=prm(S, "k_a"), scalar1=-1.0, scalar2=1.0, op0=ALU.mult, op1=ALU.add), reads=[S.prm], writes=[S.der])
    p.op("dve", lambda e: e.tensor_scalar(out=A5[:, 0:T], in0=A5[:, 0:T], scalar1=prm(S, "k_a"), scalar2=omka, op0=ALU.mult, op1=ALU.add), reads=[A5, S.prm, S.der], writes=[A5])
    p.op("dve", lambda e: e.tensor_tensor(out=k[:, 0:T], in0=k[:, 0:T], in1=A5[:, 0:T], op=ALU.mult), reads=[k, A5], writes=[k])
    p.op("dve", lambda e: e.tensor_scalar(out=A5[:, 0:T], in0=A6[:, 0:T], scalar1=-1.0, scalar2=None, op0=ALU.mult), reads=[A6], writes=[A5])
    proj_fm(S, offs, [("rlw", P, True)])
    shift("rlw", A7, 64)
    p.op("act", lambda e: e.activation(out=A7[0:64, 0:T], in_=A7[0:64, 0:T], func=AF.Tanh), reads=[A7], writes=[A7])
    heads = [((0, 64), (0, 64)), ((64, 128), (64, 128))]
    wds = [A6, P]
    for d in range(2):
        wdd = wds[d]
        for bi, (t0, n) in enumerate(BLOCKS):
            bk = S.banks[bi % 2]
            p.op("pe", lambda e, bk=bk, t0=t0, n=n, d=d: e.matmul(bk[:, :n], rw[0:64, d, :], A7[0:64, t0:t0 + n], start=True, stop=True), reads=[rw, A7], writes=[bk], skip_same=True)
            p.op("act", lambda e, bk=bk, t0=t0, n=n, d=d, wdd=wdd: e.activation(out=wdd[:, t0:t0 + n], in_=bk[:, :n], func=AF.Sigmoid, bias=prm(S, "w0_%d" % d), scale=1.0), reads=[bk, S.prm], writes=[wdd])
        p.op("act", lambda e, wdd=wdd: e.activation(out=wdd[:, 0:T], in_=wdd[:, 0:T], func=AF.Exp, scale=-math.exp(-0.5)), reads=[wdd], writes=[wdd])
    y = A7
    p.op("pool", lambda e: e.memset(y[:, 0:T], 0.0), reads=[y], writes=[y])
    g0 = scan_gen(S, 0, 128, wds[0], k, v, r, heads, (A5, A4), S.bm, 0, y, S.banks[2], S.banks[4], S.banks[5], True)
    g1 = scan_gen(S, 1, 128, wds[1], k, v, r, heads, (A5, A4), S.bm, 1, y, S.banks[3], S.banks[6], S.banks[7], True)
    run_gens([g0, g1])
    offs = load_w(S, ["rlg"])
    zero_pads(S, A6)
    proj_fm(S, offs, [("rlg", A6, True)])
    w1 = S.dcol("w1_rlg")
    p.op("dve", lambda e: e.tensor_scalar(out=w1, in0=prm(S, "mu0_rlg"), scalar1=-1.0, scalar2=1.0, op0=ALU.mult, op1=ALU.add), reads=[S.prm], writes=[S.der])
    p.op("dve", lambda e: e.tensor_tensor(out=w1, in0=w1, in1=prm(S, "mu1_rlg"), op=ALU.subtract), reads=[S.prm, S.der], writes=[S.der])
    conv3(S, A6, P, 128, prm(S, "mu0_rlg"), w1, prm(S, "mu1_rlg"))
    p.op("act", lambda e: e.activation(out=P[:, 0:T], in_=P[:, 0:T], func=AF.Sigmoid), reads=[P], writes=[P])
    p.op("dve", lambda e: e.scalar_tensor_tensor(out=A4[:, 0:T], in0=r[:, 0:T], scalar=prm(S, "r_k"), in1=k[:, 0:T], op0=ALU.mult, op1=ALU.mult), reads=[r, S.prm, k], writes=[A4])
    yv = S.io["yT"].t
    YB = S.ybase
    for bi, (t0, n) in enumerate(BLOCKS):
        bk, bk2, bk3, bk4 = S.banks[0], S.banks[1], S.banks[6], S.banks[7]
        p.op("pe", lambda e, t0=t0, n=n: e.matmul(bk[:, :n], S.bm[:], y[:, t0:t0 + n], start=True, stop=True), reads=[S.bm, y], writes=[bk], skip_same=True)
        p.op("dve", lambda e, t0=t0, n=n: e.scalar_tensor_tensor(out=t1[:, :n], in0=bk[:, :n], scalar=-1.0 / 64, in1=y[:, t0:t0 + n], op0=ALU.mult, op1=ALU.add), reads=[bk, y], writes=[t1])
        p.op("act", lambda e, n=n: e.activation(out=t2[:, :n], in_=t1[:, :n], func=AF.Square), reads=[t1], writes=[t2])
        p.op("pe", lambda e, n=n: e.matmul(bk2[:, :n], S.bm[:], t2[:, :n], start=True, stop=True), reads=[S.bm, t2], writes=[bk2], skip_same=True)
        p.op("act", lambda e, n=n: e.activation(out=t2[:, :n], in_=bk2[:, :n], func=AF.Sqrt, scale=1.0 / 64, bias=64e-5), reads=[bk2], writes=[t2])
        p.op("dve", lambda e, n=n: e.reciprocal(out=t2[:, :n], in_=t2[:, :n]), reads=[t2], writes=[t2])
        p.op("dve", lambda e, n=n: e.scalar_tensor_tensor(out=t1[:, :n], in0=t1[:, :n], scalar=prm(S, "ln_g"), in1=t2[:, :n], op0=ALU.mult, op1=ALU.mult), reads=[t1, S.prm, t2], writes=[t1])
        p.op("pe", lambda e, t0=t0, n=n: e.matmul(bk3[:, :n], S.bm[:], A4[:, t0:t0 + n], start=True, stop=True), reads=[S.bm, A4], writes=[bk3], skip_same=True)
        p.op("dve", lambda e, t0=t0, n=n: e.tensor_tensor(out=t2[:, :n], in0=bk3[:, :n], in1=v[:, t0:t0 + n], op=ALU.mult), reads=[bk3, v], writes=[t2])
        p.op("dve", lambda e, n=n: e.scalar_tensor_tensor(out=t1[:, :n], in0=t1[:, :n], scalar=prm(S, "ln_b"), in1=t2[:, :n], op0=ALU.add, op1=ALU.add), reads=[t1, S.prm, t2], writes=[t1])
        p.op("pe", lambda e, t0=t0, n=n: e.matmul(bk4[:, :n], rw[:, 3, :], P[:, t0:t0 + n], start=True, stop=True), reads=[rw, P], writes=[bk4], skip_same=True)
        p.op("dve", lambda e, n=n: e.tensor_tensor(out=S.yb[:, :n], in0=bk4[:, :n], in1=t1[:, :n], op=ALU.mult), reads=[bk4, t1], writes=[S.yb])
        p.dma("sp", lambda e, t0=t0, n=n: e.dma_start(out=yv[YB + 2, :, t0:t0 + n], in_=S.yb[:, :n]), reads=[S.yb], writes=[S.io["yT"]])


def attn(S):
    p = S.p
    A = S.arr
    q, k = A[0], A[1]
    V1 = p.alloc_sb("V1", [128, 34, 65], BF16)
    PT = [p.alloc_sb("PT%d" % i, [128, 512], BF16) for i in range(2)]
    perm = p.alloc_sb("perm", [128, 128], F32)
    p.dma("sp", lambda e: e.dma_start(out=perm[:], in_=S.io["perm"][:]), reads=[S.io["perm"]], writes=[perm])
    offs = load_w(S, ["aq", "ak", "av"])
    proj_fm(S, offs, [("aq", q, False), ("ak", k, False)])
    p.op("pool", lambda e: e.memset(V1[:], 1.0), writes=[V1])
    xv = S.io["xmT"].t.rearrange("(k p) t -> p k t", p=128)
    ov, nv = offs["av"]
    for (t0, n) in BLOCKS:
        p.dma("sp", lambda e, t0=t0, n=n: e.dma_start(out=S.xmb[:, :, :n], in_=xv[:, :, t0:t0 + n]), reads=[S.io["xmT"]], writes=[S.xmb])
        for s in range(n // 128):
            bk = S.banks[s % 2]
            for kc in range(8):
                p.op("pe", lambda e, bk=bk, kc=kc, s=s: e.matmul(bk[:, 0:64], S.xmb[:, kc, s * 128:(s + 1) * 128], S.Wb[:, kc, ov:ov + 64], start=(kc == 0), stop=(kc == 7)),
                     reads=[S.xmb, S.Wb], writes=[bk], skip_same=True)
            ti = (t0 + s * 128) // 128
            p.op("act", lambda e, bk=bk, ti=ti: e.activation(out=V1[:, ti, 0:64], in_=bk[:, 0:64], func=AF.Copy), reads=[bk], writes=[V1])
    t1, t2, t3 = S.t512
    rc, rs = S.io["ropeC"], S.io["ropeS"]
    for arr_, nname in ((q, "qn"), (k, "kn")):
        for bi, (t0, n) in enumerate(BLOCKS):
            bk = S.banks[bi % 2]
            p.op("act", lambda e, t0=t0, n=n, arr_=arr_: e.activation(out=t1[:, :n], in_=arr_[:, t0:t0 + n], func=AF.Square), reads=[arr_], writes=[t1])
            p.op("pe", lambda e, bk=bk, n=n: e.matmul(bk[:, :n], S.bm[:], t1[:, :n], start=True, stop=True), reads=[S.bm, t1], writes=[bk], skip_same=True)
            p.op("act", lambda e, bk=bk, n=n: e.activation(out=t2[:, :n], in_=bk[:, :n], func=AF.Sqrt, scale=1.0 / 64, bias=1e-6), reads=[bk], writes=[t2])
            p.op("dve", lambda e, n=n: e.reciprocal(out=t2[:, :n], in_=t2[:, :n]), reads=[t2], writes=[t2])
            p.op("dve", lambda e, t0=t0, n=n, arr_=arr_, nname=nname: e.scalar_tensor_tensor(out=arr_[:, t0:t0 + n], in0=arr_[:, t0:t0 + n], scalar=prm(S, nname), in1=t2[:, :n], op0=ALU.mult, op1=ALU.mult),
                 reads=[arr_, S.prm, t2], writes=[arr_])
            if t0 >= 256:
                x0 = t0 - 256
                p.dma("sp", lambda e, x0=x0, n=n: e.dma_start(out=t1[:, :n], in_=rc[:, x0:x0 + n]), reads=[rc], writes=[t1])
                p.dma("sp", lambda e, x0=x0, n=n: e.dma_start(out=t3[:, :n], in_=rs[:, x0:x0 + n]), reads=[rs], writes=[t3])
                bk2 = S.banks[2 + bi % 2]
                p.op("pe", lambda e, bk2=bk2, t0=t0, n=n, arr_=arr_: e.matmul(bk2[:, :n], perm[:], arr_[:, t0:t0 + n], start=True, stop=True), reads=[perm, arr_], writes=[bk2], skip_same=True)
                p.op("dve", lambda e, bk2=bk2, n=n: e.tensor_tensor(out=t3[:, :n], in0=bk2[:, :n], in1=t3[:, :n], op=ALU.mult), reads=[bk2, t3], writes=[t3])
                p.op("dve", lambda e, t0=t0, n=n, arr_=arr_: e.tensor_tensor(out=arr_[:, t0:t0 + n], in0=arr_[:, t0:t0 + n], in1=t1[:, :n], op=ALU.mult), reads=[arr_, t1], writes=[arr_])
                p.op("dve", lambda e, t0=t0, n=n, arr_=arr_: e.tensor_tensor(out=arr_[:, t0:t0 + n], in0=arr_[:, t0:t0 + n], in1=t3[:, :n], op=ALU.add), reads=[arr_, t3], writes=[arr_])
    ytm = p.alloc_sb("ytm", [128, 4, 128], F32)
    rec = p.alloc_sb("rec", [128, 4], F32)
    yv = S.io["yT"].t
    YB = S.ybase
    pp = 0
    for (t0, n) in BLOCKS:
        kchunks = [0, 1] if t0 < 256 else list(range(34))
        nq = n // 128
        for h in range(2):
            hs = slice(64 * h, 64 * h + 64)
            for ki, kc in enumerate(kchunks):
                sb = S.banks[pp % 2]
                pt = PT[pp % 2]
                pp += 1
                p.op("pe", lambda e, sb=sb, kc=kc, t0=t0, n=n, hs=hs: e.matmul(sb[:, :n], k[hs, kc * 128:(kc + 1) * 128], q[hs, t0:t0 + n], start=True, stop=True), reads=[k, q], writes=[sb], skip_same=True)
                p.op("act", lambda e, sb=sb, pt=pt, n=n: e.activation(out=pt[:, :n], in_=sb[:, :n], func=AF.Exp, scale=0.125), reads=[sb], writes=[pt])
                for qs in range(nq):
                    ob = S.banks[2 + qs]
                    p.op("pe", lambda e, ob=ob, pt=pt, qs=qs, kc=kc, ki=ki: e.matmul(ob[:, 0:65], pt[:, qs * 128:(qs + 1) * 128], V1[:, kc, :], start=(ki == 0), stop=(ki == len(kchunks) - 1)),
                         reads=[pt, V1], writes=[ob], skip_same=True)
            for qs in range(nq):
                ob = S.banks[2 + qs]
                p.op("dve", lambda e, ob=ob, qs=qs: e.reciprocal(out=rec[:, qs:qs + 1], in_=ob[:, 64:65]), reads=[ob], writes=[rec])
                p.op("dve", lambda e, ob=ob, qs=qs, h=h: e.tensor_scalar(out=ytm[:, qs, 64 * h:64 * h + 64], in0=ob[:, 0:64], scalar1=rec[:, qs:qs + 1], scalar2=None, op0=ALU.mult), reads=[ob, rec], writes=[ytm])
        tb = S.banks[6]
        for qs in range(nq):
            p.op("pe", lambda e, qs=qs: e.transpose(tb[:, qs * 128:(qs + 1) * 128], ytm[:, qs, :], S.ident[:]), reads=[ytm, S.ident], writes=[tb], skip_same=True)
        p.op("act", lambda e, n=n: e.activation(out=S.yb[:, :n], in_=tb[:, :n], func=AF.Copy), reads=[tb], writes=[S.yb])
        p.dma("sp", lambda e, t0=t0, n=n: e.dma_start(out=yv[YB + 3, :, t0:t0 + n], in_=S.yb[:, :n]), reads=[S.yb], writes=[S.io["yT"]])


def wout_stage(S, NM=4):
    p = S.p
    Wo = p.alloc_sb("Wo", [128, NM, 1024], BF16)
    yblk = p.alloc_sb("yblk", [128, NM, 512], BF16)
    for m in range(NM):
        p.dma("sp", lambda e, m=m: e.dma_start(out=S.stage[:, 0:1024], in_=S.io["wo"][m * 128:(m + 1) * 128, :]), reads=[S.io["wo"]], writes=[S.stage])
        p.op("pool", lambda e, m=m: e.tensor_copy(out=Wo[:, m, :], in_=S.stage[:, 0:1024]), reads=[S.stage], writes=[Wo])
    yv = S.io["yT"].t.rearrange("m p t -> p m t")
    pv = S.io["PT"].t.rearrange("(k p) t -> p k t", p=128)
    ob = 0
    for (t0, n) in BLOCKS:
        p.dma("sp", lambda e, t0=t0, n=n: e.dma_start(out=yblk[:, :, :n], in_=yv[:, :, t0:t0 + n]), reads=[S.io["yT"]], writes=[yblk])
        for dc in range(8):
            bk = S.banks[ob % 2]
            tt = S.t512[ob % 2]
            ob += 1
            for m in range(NM):
                p.op("pe", lambda e, bk=bk, m=m, dc=dc, n=n: e.matmul(bk[:, :n], Wo[:, m, dc * 128:(dc + 1) * 128], yblk[:, m, :n], start=(m == 0), stop=(m == NM - 1)), reads=[Wo, yblk], writes=[bk], skip_same=True)
            p.op("act", lambda e, bk=bk, tt=tt, n=n: e.activation(out=tt[:, :n], in_=bk[:, :n], func=AF.Copy), reads=[bk], writes=[tt])
            p.dma("sp", lambda e, tt=tt, dc=dc, t0=t0, n=n: e.dma_start(out=pv[:, dc, t0:t0 + n], in_=tt[:, :n]), reads=[tt], writes=[S.io["PT"]])


def wout_light(p, io, banks, NM):
    stage = p.alloc_sb("wstage", [128, 1024], F32)
    tt2 = [p.alloc_sb("wt%d" % i, [128, 512], F32) for i in range(2)]
    Wo = p.alloc_sb("Wo", [128, NM, 1024], BF16)
    yblk = p.alloc_sb("yblk", [128, NM, 512], BF16)
    for m in range(NM):
        p.dma("sp", lambda e, m=m: e.dma_start(out=stage[:, 0:1024], in_=io["wo"][m * 128:(m + 1) * 128, :]), reads=[io["wo"]], writes=[stage])
        p.op("pool", lambda e, m=m: e.tensor_copy(out=Wo[:, m, :], in_=stage[:, 0:1024]), reads=[stage], writes=[Wo])
    yv = io["yT"].t.rearrange("m p t -> p m t")
    pv = io["PT"].t.rearrange("(k p) t -> p k t", p=128)
    ob = 0
    for (t0, n) in BLOCKS:
        p.dma("sp", lambda e, t0=t0, n=n: e.dma_start(out=yblk[:, :, :n], in_=yv[:, :, t0:t0 + n]), reads=[io["yT"]], writes=[yblk])
        for dc in range(8):
            bk = banks[ob % 2]
            tt = tt2[ob % 2]
            ob += 1
            for m in range(NM):
                p.op("pe", lambda e, bk=bk, m=m, dc=dc, n=n: e.matmul(bk[:, :n], Wo[:, m, dc * 128:(dc + 1) * 128], yblk[:, m, :n], start=(m == 0), stop=(m == NM - 1)), reads=[Wo, yblk], writes=[bk], skip_same=True)
            p.op("act", lambda e, bk=bk, tt=tt, n=n: e.activation(out=tt[:, :n], in_=bk[:, :n], func=AF.Copy), reads=[bk], writes=[tt])
            p.dma("sp", lambda e, tt=tt, dc=dc, t0=t0, n=n: e.dma_start(out=pv[:, dc, t0:t0 + n], in_=tt[:, :n]), reads=[tt], writes=[io["PT"]])


def build_k2(mixers=("gla", "ssd", "rwkv", "att"), test=True):
    nc = bass.Bass("TRN2", target_bir_lowering=False)
    p = Prog(nc)
    io = {}
    io["xmT"] = p.dram("xmT", [1024, T], BF16, "ExternalInput")
    io["wsel"] = p.dram("wsel", [1024, NW], F32, "ExternalInput")
    io["prm"] = p.dram("prm", [128, NPRM], F32, "ExternalInput")
    io["gwdec"] = p.dram("gwdec", [16, 2, 64], F32, "ExternalInput")
    io["rww"] = p.dram("rww", [128, 4, 128], F32, "ExternalInput")
    io["perm"] = p.dram("perm", [128, 128], F32, "ExternalInput")
    io["ropeC"] = p.dram("ropeC", [128, 4096], F32, "ExternalInput")
    io["ropeS"] = p.dram("ropeS", [128, 4096], F32, "ExternalInput")
    io["wo"] = p.dram("wo", [512, 1024], F32, "ExternalInput")
    io["yT"] = p.dram("yT", [4, 128, T], BF16, "ExternalOutput" if test else "Internal")
    io["PT"] = p.dram("PT", [1024, T], F32, "ExternalOutput")
    io["ssq0"] = p.dram("ssq0", [128, T], F32)
    io["yg0"] = p.dram("yg0", [128, T], F32)
    banks = [p.alloc_ps("bank%d" % i, [128, 512], F32) for i in range(8)]
    mk0 = p.mark()
    S = k2_setup(p, io, banks)
    S.ssd_both = False
    der_setup(S)
    if "gla" in mixers:
        gla(S)
    if "ssd" in mixers:
        ssd(S)
    if "rwkv" in mixers:
        rwkv(S)
    if "att" in mixers:
        attn(S)
    if "wout" in mixers:
        p.release(mk0)
        wout_light(p, io, banks, 4)
    p.finish()
    return nc

import numpy as np


def wsel_for(inp, l, hp):
    W = inp['w_in'][l]
    cols = {}
    cols["gq"] = np.arange(hp * 64, hp * 64 + 64)
    cols["gk"] = 128 + cols["gq"]
    cols["gv"] = 256 + np.arange(hp * 128, hp * 128 + 128)
    cols["gg"] = 512 + np.arange(hp * 128, hp * 128 + 128)
    cols["glr"] = 768 + np.arange(16)
    sb = 784
    for s in range(2):
        g = (1 - hp) if s == 0 else hp
        cols["sz%d" % s] = sb + np.arange(g * 128, g * 128 + 128)
        cols["sx%d" % s] = sb + 256 + np.arange(g * 128, g * 128 + 128)
        cols["sB%d" % s] = np.tile(sb + 256 + 256 + np.arange(g * 64, g * 64 + 64), 2)
        cols["sC%d" % s] = np.tile(sb + 256 + 384 + np.arange(g * 64, g * 64 + 64), 2)
        cols["sdt%d" % s] = sb + 768 + np.repeat(np.arange(2 * g, 2 * g + 2), 64)
    rb = 784 + 772
    cols["rr"] = rb + np.arange(hp * 128, hp * 128 + 128)
    cols["rk"] = rb + 256 + np.arange(hp * 128, hp * 128 + 128)
    cols["rv"] = rb + 512 + np.arange(hp * 128, hp * 128 + 128)
    cols["rlw"] = rb + 768 + np.arange(64)
    cols["rla"] = rb + 832 + np.arange(64)
    cols["rlg"] = rb + 896 + np.arange(128)
    ab = rb + 1024
    cols["aq"] = ab + np.arange(hp * 128, hp * 128 + 128)
    cols["ak"] = np.tile(ab + 256 + np.arange(hp * 64, hp * 64 + 64), 2)
    cols["av"] = ab + 384 + np.arange(hp * 64, hp * 64 + 64)
    out = np.zeros((1024, NW), np.float32)
    for nm, (c0, n) in WCOLS.items():
        out[:, c0:c0 + n] = W[:, cols[nm]]
    return out


def prm_for(inp, l, hp):
    P = np.zeros((128, NPRM), np.float32)

    def put(nm, v):
        v = np.asarray(v).reshape(-1)
        P[:len(v), PRM[nm]] = v
    put("gb0", inp['gla_b_dec'][l, 0, hp * 64:hp * 64 + 64])
    put("gb1", inp['gla_b_dec'][l, 1, hp * 64:hp * 64 + 64])
    put("gnorm", np.tile(inp['gla_norm'][l], 2))
    cw, cb = inp['ssm_conv_w'][l], inp['ssm_conv_b'][l]
    for s in range(2):
        g = (1 - hp) if s == 0 else hp
        chs = {"sx": np.arange(g * 128, g * 128 + 128), "sB": np.tile(256 + np.arange(g * 64, g * 64 + 64), 2), "sC": np.tile(384 + np.arange(g * 64, g * 64 + 64), 2)}
        for nm, ch in chs.items():
            for j in range(3):
                put("cw%d_%s%d" % (j, nm, s), cw[j, ch])
            put("cb_%s%d" % (nm, s), cb[ch])
        hd = np.repeat(np.arange(2 * g, 2 * g + 2), 64)
        for d in range(2):
            put("dtb%d_%d" % (d, s), inp['ssm_dt_bias'][l, d, hd])
            put("alog%d_%d" % (d, s), inp['ssm_a_log'][l, d, hd])
        put("D_%d" % s, inp['ssm_d'][l, hd])
    put("snorm", inp['ssm_norm'][l, hp * 128:hp * 128 + 128])
    put("snorm_o", inp['ssm_norm'][l, (1 - hp) * 128:(1 - hp) * 128 + 128])
    mu = inp['rwkv_shift_mu'][l]
    rc = {"rr": np.arange(hp * 128, hp * 128 + 128), "rk": 256 + np.arange(hp * 128, hp * 128 + 128), "rv": 512 + np.arange(hp * 128, hp * 128 + 128),
          "rlw": 768 + np.arange(64), "rla": 832 + np.arange(64), "rlg": 896 + np.arange(128)}
    for nm, ch in rc.items():
        put("mu0_" + nm, mu[0, ch])
        put("mu1_" + nm, mu[1, ch])
    own = slice(hp * 128, hp * 128 + 128)
    put("w0_0", inp['rwkv_w0'][l, 0, own])
    put("w0_1", inp['rwkv_w0'][l, 1, own])
    put("a0", inp['rwkv_a0'][l, own])
    put("k_k", inp['rwkv_k_k'][l, own])
    put("k_a", inp['rwkv_k_a'][l, own])
    put("r_k", inp['rwkv_r_k'][l, 2 * hp:2 * hp + 2])
    put("ln_g", inp['rwkv_ln_g'][l, own])
    put("ln_b", inp['rwkv_ln_b'][l, own])
    put("qn", np.tile(inp['att_q_norm'][l], 2))
    put("kn", np.tile(inp['att_k_norm'][l], 2))
    return P


def misc_for(inp, l, hp):
    m = {}
    m["gwdec"] = np.ascontiguousarray(inp['gla_w_dec'][l][:, :, hp * 64:hp * 64 + 64].transpose(1, 0, 2))
    own = slice(hp * 128, hp * 128 + 128)
    rww = np.zeros((128, 4, 128), np.float32)
    rww[0:64, 0] = inp['rwkv_w_dec'][l, 0][:, own]
    rww[0:64, 1] = inp['rwkv_w_dec'][l, 1][:, own]
    rww[0:64, 2] = inp['rwkv_w_a'][l][:, own]
    rww[:, 3] = inp['rwkv_w_g'][l][:, own]
    m["rww"] = rww
    Wo = inp['w_out'][l]
    m["wo"] = np.concatenate([Wo[0 + hp * 128:0 + hp * 128 + 128], Wo[256 + hp * 128:256 + hp * 128 + 128],
                              Wo[512 + hp * 128:512 + hp * 128 + 128], Wo[768 + hp * 128:768 + hp * 128 + 128]], 0)
    return m


def rope_consts():
    t = np.arange(4096)
    pos = np.stack([(t // 64).astype(np.float32), (t % 64).astype(np.float32)], 0)
    inv = (np.float32(10000.0) ** (-np.arange(0, 32, 2, dtype=np.float32) / np.float32(32))).astype(np.float32)
    C = np.zeros((128, 4096), np.float32)
    Sg = np.zeros((128, 4096), np.float32)
    perm = np.zeros((128, 128), np.float32)
    for h in range(2):
        for ax in range(2):
            for half in range(2):
                for i in range(16):
                    d = h * 64 + ax * 32 + half * 16 + i
                    ang = (pos[ax] * inv[i]).astype(np.float32)
                    C[d] = np.cos(ang)
                    Sg[d] = -np.sin(ang) if half == 0 else np.sin(ang)
                    partner = d + 16 if half == 0 else d - 16
                    perm[partner, d] = 1.0
    return {"ropeC": C, "ropeS": Sg, "perm": perm}


_PROGS = {}


def _prog(key, fn):
    if key not in _PROGS:
        _PROGS[key] = fn()
    return _PROGS[key]


def _core_tok_T(x, ctx, b, h):
    return np.ascontiguousarray(np.concatenate([ctx[b, h * 128:(h + 1) * 128], x[b, h * 2048:(h + 1) * 2048]], 0).T)


def _pt(v, n):
    return np.ascontiguousarray(np.asarray(v).reshape(n, 128).T)


def kernel(**inp):
    inp = {k: np.asarray(v) for k, v in inp.items()}
    x, ctx = inp['x'], inp['ctx']
    cores = list(range(8))
    xT = [_core_tok_T(x, ctx, c // 2, c % 2) for c in cores]
    rc = rope_consts()
    for l in range(4):
        base = []
        for c in cores:
            b = c // 2
            cT = np.stack([inp['c_ctx'], inp['c'][b]], -1)
            m = {"cT": np.ascontiguousarray(cT.reshape(8, 128, 2).transpose(1, 0, 2).reshape(128, 16)),
                 "bmT": _pt(inp['b_mod'][l], 72),
                 "ngT": np.ascontiguousarray(inp['norm_g'][l].reshape(3, 8, 128).transpose(2, 0, 1).reshape(128, 24)),
                 "w_mod": inp['w_mod'][l]}
            base.append(m)
        nca = _prog("k1a", lambda: build_k1(False, True, 0))
        maps = []
        for c in cores:
            m = dict(base[c])
            m["xT"] = xT[c]
            m["ffn_in"] = inp['ffn_in'][l, 0]
            m["ffn_out"] = inp['ffn_out'][l, 0]
            maps.append(m)
        res = run_bass_kernel_spmd(nca, maps, core_ids=cores).results
        xT = [res[c]["yT"] for c in cores]
        xm = [res[c]["xmT"] for c in cores]
        nc2 = _prog("k2", lambda: build_k2(("gla", "ssd", "rwkv", "att", "wout"), test=False))
        maps = []
        for c in cores:
            b, hp = c // 2, c % 2
            c0, c1 = xm[2 * b], xm[2 * b + 1]
            full = np.ascontiguousarray(np.concatenate([c0[:, :128], c1[:, :128], c0[:, 128:], c1[:, 128:]], axis=1))
            m = {"xmT": full, "wsel": wsel_for(inp, l, hp), "prm": prm_for(inp, l, hp)}
            m.update(misc_for(inp, l, hp))
            m.update(rc)
            maps.append(m)
        res = run_bass_kernel_spmd(nc2, maps, core_ids=cores).results
        PT = [res[c]["PT"] for c in cores]
        ncb = _prog("k1b", lambda: build_k1(True, False, 1))
        maps = []
        for c in cores:
            b, h = c // 2, c % 2
            m = dict(base[c])
            m["xT"] = xT[c]
            m["ffn_in"] = inp['ffn_in'][l, 1]
            m["ffn_out"] = inp['ffn_out'][l, 1]
            for j in range(2):
                Pj = PT[2 * b + j]
                m["P%d" % j] = np.ascontiguousarray(np.concatenate([Pj[:, h * 128:(h + 1) * 128], Pj[:, 256 + h * 2048:256 + (h + 1) * 2048]], axis=1))
            maps.append(m)
        res = run_bass_kernel_spmd(ncb, maps, core_ids=cores).results
        xT = [res[c]["yT"] for c in cores]
    out = np.zeros((4, 4096, 1024), np.float32)
    for c in cores:
        b, h = c // 2, c % 2
        out[b, h * 2048:(h + 1) * 2048] = xT[c][:, 128:].T
    return out
```
